# Optimizing a Trainium2 kernel written in Bass

```python
import jax, jax.numpy as jnp
from jax import lax
import numpy as np

D_MODEL = 1024
BATCH = 4
SEQ = 8192
DEPTH = 1

CONV_WIDTH = 512
CONV_GROUPS = 8
CONV_KERNEL = 3
N_Q_HEADS = 8
N_KV_HEADS = 2
HEAD_DIM = 64
ATTN_WIDTH = N_Q_HEADS * HEAD_DIM
KV_WIDTH = N_KV_HEADS * HEAD_DIM
WINDOW = 128
BLOCK = 128
ROPE_THETA = 500000.0
ROT_DIM = HEAD_DIM // 4
MIX_WIDTH = CONV_WIDTH + ATTN_WIDTH
IN_PROJ_WIDTH = 3 * CONV_WIDTH + ATTN_WIDTH + 2 * KV_WIDTH
D_FF = 2816
FFN_RES_SCALE = 0.5
RMS_EPS = 1e-5
MASK_VALUE = -1e30

kernel_name = "hybrid_shortconv_swa_sink_macaron"


def rms_norm(x, gain):
    xf = x.astype(jnp.float32)
    inv = lax.rsqrt(jnp.mean(xf * xf, axis=-1, keepdims=True) + RMS_EPS)
    return (xf * inv).astype(x.dtype) * gain


def swiglu(h, w_gate, w_up, w_down):
    return (jax.nn.silu(h @ w_gate) * (h @ w_up)) @ w_down


def partial_rotary(t, seq_len):
    half = ROT_DIM // 2
    inv_freq = ROPE_THETA ** (-jnp.arange(0, ROT_DIM, 2, dtype=jnp.float32) / ROT_DIM)
    ang = jnp.arange(seq_len, dtype=jnp.float32)[:, None] * inv_freq[None, :]
    cos = jnp.cos(ang)[None, :, None, :].astype(t.dtype)
    sin = jnp.sin(ang)[None, :, None, :].astype(t.dtype)
    t1, t2, t_pass = t[..., :half], t[..., half:ROT_DIM], t[..., ROT_DIM:]
    return jnp.concatenate([t1 * cos - t2 * sin, t2 * cos + t1 * sin, t_pass], axis=-1)


def short_conv_mixer(b_gate, c_gate, u, conv_w):
    v = c_gate * u
    y = lax.conv_general_dilated(
        v, conv_w[:, None, :], window_strides=(1,), padding=[(CONV_KERNEL - 1, 0)],
        dimension_numbers=('NWC', 'WIO', 'NWC'), feature_group_count=CONV_WIDTH)
    return b_gate * y


def sliding_window_sink_attention(q, k, v, sinks):
    b, s = q.shape[0], q.shape[1]
    nb = s // BLOCK
    g = N_Q_HEADS // N_KV_HEADS
    qb = q.reshape(b, nb, BLOCK, N_KV_HEADS, g, HEAD_DIM)

    def band(t):
        tp = jnp.pad(t, ((0, 0), (BLOCK, 0), (0, 0), (0, 0))).reshape(b, nb + 1, BLOCK, N_KV_HEADS, HEAD_DIM)
        return jnp.concatenate([tp[:, :-1], tp[:, 1:]], axis=2)

    kb, vb = band(k), band(v)
    scores = jnp.einsum('bnqhgd,bnkhd->bnhgqk', qb, kb).astype(jnp.float32) * (HEAD_DIM ** -0.5)

    qi = jnp.arange(BLOCK)[:, None]
    kj = jnp.arange(2 * BLOCK)[None, :]
    rel = kj - BLOCK - qi
    in_window = (rel <= 0) & (rel > -WINDOW)
    blk = jnp.arange(nb)[:, None, None]
    k_exists = (blk * BLOCK + kj[None] - BLOCK) >= 0
    mask = (in_window[None] & k_exists)[None, :, None, None]
    scores = jnp.where(mask, scores, MASK_VALUE)

    sink = sinks.astype(jnp.float32).reshape(N_KV_HEADS, g)[None, None, :, :, None, None]
    m = jnp.maximum(jnp.max(scores, axis=-1, keepdims=True), sink)
    p = jnp.exp(scores - m)
    probs = p / (jnp.sum(p, axis=-1, keepdims=True) + jnp.exp(sink - m))
    out = jnp.einsum('bnhgqk,bnkhd->bnqhgd', probs.astype(vb.dtype), vb)
    return out.reshape(b, s, ATTN_WIDTH)


def setup_inputs(seed: int = 0) -> dict:
    key = jax.random.key(seed)
    ks = jax.random.split(key, 16)
    f32 = jnp.float32

    def w(k, shape, fan_in):
        return jax.random.normal(k, shape, f32) * (fan_in ** -0.5)

    def gain(k):
        return 1.0 + 0.02 * jax.random.normal(k, (DEPTH, D_MODEL), f32)

    return {
        "x": jax.random.normal(ks[0], (BATCH, SEQ, D_MODEL), f32),
        "ffn1_norm": gain(ks[1]),
        "ffn1_w_gate": w(ks[2], (DEPTH, D_MODEL, D_FF), D_MODEL),
        "ffn1_w_up": w(ks[3], (DEPTH, D_MODEL, D_FF), D_MODEL),
        "ffn1_w_down": w(ks[4], (DEPTH, D_FF, D_MODEL), D_FF),
        "mix_norm": gain(ks[5]),
        "w_in": w(ks[6], (DEPTH, D_MODEL, IN_PROJ_WIDTH), D_MODEL),
        "conv_w": w(ks[7], (DEPTH, CONV_KERNEL, CONV_WIDTH), CONV_KERNEL),
        "attn_sinks": 0.5 * jax.random.normal(ks[8], (DEPTH, N_Q_HEADS), f32),
        "w_out": w(ks[9], (DEPTH, MIX_WIDTH, D_MODEL), MIX_WIDTH),
        "ffn2_norm": gain(ks[10]),
        "ffn2_w_gate": w(ks[11], (DEPTH, D_MODEL, D_FF), D_MODEL),
        "ffn2_w_up": w(ks[12], (DEPTH, D_MODEL, D_FF), D_MODEL),
        "ffn2_w_down": w(ks[13], (DEPTH, D_FF, D_MODEL), D_FF),
        "final_norm": 1.0 + 0.02 * jax.random.normal(ks[14], (D_MODEL,), f32),
    }


def reference(x, ffn1_norm, ffn1_w_gate, ffn1_w_up, ffn1_w_down, mix_norm, w_in, conv_w,
              attn_sinks, w_out, ffn2_norm, ffn2_w_gate, ffn2_w_up, ffn2_w_down, final_norm):
    b, s, _ = x.shape
    splits = np.cumsum([CONV_WIDTH, CONV_WIDTH, CONV_WIDTH, ATTN_WIDTH, KV_WIDTH]).tolist()
    for l in range(DEPTH):
        x = x + FFN_RES_SCALE * swiglu(rms_norm(x, ffn1_norm[l]), ffn1_w_gate[l], ffn1_w_up[l], ffn1_w_down[l])

        h = rms_norm(x, mix_norm[l])
        z = h @ w_in[l]
        b_gate, c_gate, u, q, k, v = jnp.split(z, splits, axis=-1)

        y_conv = short_conv_mixer(b_gate, c_gate, u, conv_w[l])

        q = partial_rotary(q.reshape(b, s, N_Q_HEADS, HEAD_DIM), s)
        k = partial_rotary(k.reshape(b, s, N_KV_HEADS, HEAD_DIM), s)
        v = v.reshape(b, s, N_KV_HEADS, HEAD_DIM)
        y_attn = sliding_window_sink_attention(q, k, v, attn_sinks[l])

        x = x + jnp.concatenate([y_conv, y_attn], axis=-1) @ w_out[l]

        x = x + FFN_RES_SCALE * swiglu(rms_norm(x, ffn2_norm[l]), ffn2_w_gate[l], ffn2_w_up[l], ffn2_w_down[l])
    return rms_norm(x, final_norm)
```

```python
import contextlib
import numpy as np
import ml_dtypes
import concourse.bass as bass
import concourse.mybir as mybir
from concourse.bass_utils import run_bass_kernel_spmd

F32 = mybir.dt.float32
BF16 = mybir.dt.bfloat16
AF = mybir.ActivationFunctionType
ALU = mybir.AluOpType

D = 1024
DFF = 2816
NCORES = 8
TOK_CORE = 4096
HALO = 128
TILE = 512
NT_FULL = TOK_CORE // TILE
GROUPS = [4, 4, 4, 4, 4, 2]
EPS = 1e-5
SCALE = 0.125
ROPE_THETA = 500000.0
WINP_COLS = 256 + 512 + 4 * 384


class Sem:
    def __init__(self, h):
        self.h = h
        self.count = 0


class Engine:
    def __init__(self, name, inst, sem):
        self.name = name
        self.i = inst
        self.sem = sem
        self.waited = {}


class Rec:
    __slots__ = ("box", "sem", "cnt", "w")

    def __init__(self, box, sem, cnt, w):
        self.box, self.sem, self.cnt, self.w = box, sem, cnt, w


class Buf:
    def __init__(self, name, t, shape):
        self.name = name
        self.t = t
        self.shape = tuple(shape)
        self.recs = []

    def v(self, *idx):
        idx = list(idx) + [slice(None)] * (len(self.shape) - len(idx))
        box = []
        for d, k in zip(self.shape, idx):
            if isinstance(k, int):
                box.append((k, k + 1))
            else:
                a = 0 if k.start is None else k.start
                b = d if k.stop is None else k.stop
                assert 0 <= a < b <= d, (self.name, idx, self.shape)
                box.append((a, b))
        ap = self.t[tuple(idx)] if self.t is not None else None
        return View(self, tuple(box), ap)


class View:
    def __init__(self, buf, box, ap):
        self.buf, self.box, self.ap = buf, box, ap

    def with_ap(self, ap):
        return View(self.buf, self.box, ap)


def _overlap(a, b):
    for (a0, a1), (b0, b1) in zip(a, b):
        if a1 <= b0 or b1 <= a0:
            return False
    return True


def _contains(a, b):
    for (a0, a1), (b0, b1) in zip(a, b):
        if b0 < a0 or b1 > a1:
            return False
    return True


class Prog:
    def __init__(self, nc, es):
        self.nc = nc
        self.es = es
        self.nsem = 0
        self.pe = Engine("pe", nc.tensor, self.new_sem("pe"))
        self.act = Engine("act", nc.scalar, self.new_sem("act"))
        self.dve = Engine("dve", nc.vector, self.new_sem("dve"))
        self.pool = Engine("pool", nc.gpsimd, self.new_sem("pool"))
        self.sp = Engine("sp", nc.sync, self.new_sem("sp"))
        self.sbuf_bytes = 0

    def new_sem(self, name):
        self.nsem += 1
        return Sem(self.es.enter_context(self.nc.semaphore(f"s_{name}_{self.nsem}")))

    def sbuf(self, name, shape, dt):
        t = self.es.enter_context(self.nc.sbuf_tensor("sb_" + name, list(shape), dt))
        n = 1
        for s in shape[1:]:
            n *= s
        self.sbuf_bytes += n * (4 if dt == F32 else 2)
        return Buf(name, t, shape)

    def psum(self, name, shape, dt):
        t = self.es.enter_context(self.nc.psum_tensor("ps_" + name, list(shape), dt))
        b = Buf(name, t, shape)
        b.is_psum = True
        return b

    def _sync(self, eng, reads, writes):
        need = {}
        for v in reads:
            psum = getattr(v.buf, "is_psum", False)
            for r in v.buf.recs:
                if (r.w and (psum or _overlap(r.box, v.box))) or (psum and (not r.w) and r.sem is not eng.sem):
                    if need.get(r.sem, 0) < r.cnt:
                        need[r.sem] = r.cnt
        for v in writes:
            psum = getattr(v.buf, "is_psum", False)
            for r in v.buf.recs:
                if psum or _overlap(r.box, v.box):
                    if need.get(r.sem, 0) < r.cnt:
                        need[r.sem] = r.cnt
        for s, c in need.items():
            if s is eng.sem and eng.name == "pe":
                continue
            if eng.waited.get(s, 0) >= c:
                continue
            eng.i.wait_ge(s.h, c)
            eng.waited[s] = c

    def _record(self, reads, writes, sem, cnt):
        for v in reads:
            recs = v.buf.recs
            recs[:] = [r for r in recs if not ((not r.w) and r.sem is sem and r.box == v.box)]
            recs.append(Rec(v.box, sem, cnt, False))
        for v in writes:
            recs = v.buf.recs
            recs[:] = [r for r in recs if not _contains(v.box, r.box)]
            recs.append(Rec(v.box, sem, cnt, True))

    def op(self, eng, reads, writes, fn):
        self._sync(eng, reads, writes)
        ins = fn()
        eng.sem.count += 1
        ins.then_inc(eng.sem.h, 1)
        self._record(reads, writes, eng.sem, eng.sem.count)

    def mm_group(self, out, pairs, transpose_ident=None):
        eng = self.pe
        reads = []
        for a, b in pairs:
            reads += [a, b]
        if transpose_ident is not None:
            reads.append(transpose_ident)
        self._sync(eng, reads, [out])
        n = len(pairs)
        ins = None
        for i, (a, b) in enumerate(pairs):
            ins = self.nc.tensor.matmul(out.ap, a.ap, b.ap, start=(i == 0), stop=(i == n - 1))
        eng.sem.count += 1
        ins.then_inc(eng.sem.h, 1)
        self._record(reads, [out], eng.sem, eng.sem.count)

    def tr_group(self, outs_ins, ident, outbox):
        eng = self.pe
        reads = [i for _, i in outs_ins] + [ident]
        self._sync(eng, reads, [outbox])
        ins = None
        for o, i in outs_ins:
            ins = self.nc.tensor.transpose(o.ap, i.ap, ident.ap)
        eng.sem.count += 1
        ins.then_inc(eng.sem.h, 1)
        self._record(reads, [outbox], eng.sem, eng.sem.count)

    def dma(self, qeng, out_ap, in_ap, sem, reads, writes):
        self._sync(qeng, reads, writes)
        ins = qeng.i.dma_start(out=out_ap, in_=in_ap)
        sem.count += 16
        ins.then_inc(sem.h, 16)
        self._record(reads, writes, sem, sem.count)


def dma_group(P, qeng, items, sem):
    for (out_ap, in_ap, reads, writes) in items:
        P._sync(qeng, reads, writes)
    for (out_ap, in_ap, reads, writes) in items:
        ins = qeng.i.dma_start(out=out_ap, in_=in_ap)
        sem.count += 16
        ins.then_inc(sem.h, 16)
    for (out_ap, in_ap, reads, writes) in items:
        P._record(reads, writes, sem, sem.count)


class Ring:
    def __init__(self, bufs):
        self.bufs = bufs
        self.n = 0

    def next(self):
        b = self.bufs[self.n % len(self.bufs)]
        self.n += 1
        return b


def build_program(n_tiles=NT_FULL):
    nc = bass.Bass("TRN2", target_bir_lowering=False)

    def din(name, shape, dt=F32):
        return nc.dram_tensor(name, list(shape), dt, kind="ExternalInput").ap()

    xin = din("xin", [HALO + TOK_CORE, D])
    w_ffn = {}
    for k in (1, 2):
        w_ffn[k] = (din(f"wg{k}", [D, DFF]), din(f"wu{k}", [D, DFF]), din(f"wd{k}", [DFF, D]))
    winp = din("winp", [D, WINP_COLS])
    wout = din("wout", [D, D])
    gcols_d = din("gcols", [128, 3, 8])
    gfin_d = din("gfin", [1, D])
    convw_d = din("convw", [128, 4, 3])
    sinks_d = din("sinks", [128, 4])
    cos_d = din("cos_t", [128, HALO + TOK_CORE])
    sin_d = din("sin_t", [128, HALO + TOK_CORE])
    masks_d = din("masks", [128, 3, 512], BF16)
    ident_d = din("ident", [128, 128], BF16)
    perm_d = din("permT", [128, 128], BF16)
    out_d = nc.dram_tensor("out", [TOK_CORE, D], F32, kind="ExternalOutput").ap()

    with contextlib.ExitStack() as es:
        P = Prog(nc, es)
        pe, act, dve, pool, sp = P.pe, P.act, P.dve, P.pool, P.sp

        jobs = []

        def add_job(name, ring, shape, srcs):
            scr = nc.dram_tensor(f"scr_{name}", list(shape), BF16, kind="Internal").ap()
            jobs.append(dict(name=name, ring=ring, shape=shape, srcs=srcs, scr=scr,
                             buf=Buf(f"scr_{name}", None, (1,))))
            return len(jobs) - 1

        ffn_jobs = {}
        for k in (1, 2):
            wg, wu, wd = w_ffn[k]
            wgv = wg.rearrange("(kc p) f -> p kc f", p=128)
            wuv = wu.rearrange("(kc p) f -> p kc f", p=128)
            wdv = wd.rearrange("(j p) d -> p j d", p=128)
            lst = []
            c0 = 0
            for gi, nch in enumerate(GROUPS):
                cols = nch * 128
                jg = add_job(f"g{k}_{gi}", "A", (128, 8, cols), [((0, 128), wgv[:, :, c0 * 128:c0 * 128 + cols])])
                ju = add_job(f"u{k}_{gi}", "A", (128, 8, cols), [((0, 128), wuv[:, :, c0 * 128:c0 * 128 + cols])])
                jd = add_job(f"d{k}_{gi}", "D", (128, nch, D), [((0, 128), wdv[:, c0:c0 + nch, :])])
                lst.append((jg, ju, jd, nch))
                c0 += nch
            ffn_jobs[k] = lst
        winv = winp.rearrange("(kc p) f -> p kc f", p=128)
        mix_jobs = []
        mcols = [256, 512, 384, 384, 384, 384]
        c0 = 0
        for mi, cols in enumerate(mcols):
            mix_jobs.append(add_job(f"m{mi}", "A", (128, 8, cols), [((0, 128), winv[:, :, c0:c0 + cols])]))
            c0 += cols
        assert c0 == WINP_COLS
        woc = wout[0:512, :].rearrange("(j p) d -> p j d", p=128)
        job_oc = add_job("oc", "D", (128, 4, D), [((0, 128), woc)])
        wo_a0 = wout[512:768, :].rearrange("(c p) d -> p c d", p=64)
        wo_a1 = wout[768:1024, :].rearrange("(c p) d -> p c d", p=64)
        job_oa = add_job("oa", "D", (128, 4, D), [((0, 64), wo_a0), ((64, 128), wo_a1)])

        order = []
        for k in (1, 2):
            lst = ffn_jobs[k]
            seq = []
            for gi, (jg, ju, jd, nch) in enumerate(lst):
                seq += [jg, ju]
                if gi > 0:
                    seq.append(lst[gi - 1][2])
            seq.append(lst[-1][2])
            if k == 1:
                order += seq + mix_jobs + [job_oc, job_oa]
            else:
                order += seq
        for j in order:
            jobs[j]["csem"] = P.new_sem(f"cv{j}")

        ringA = [P.sbuf(f"rA{i}", (128, 8, 512), BF16) for i in range(4)]
        ringD = [P.sbuf(f"rD{i}", (128, 4, D), BF16) for i in range(2)]
        ring_sems = {id(b): P.new_sem("ld") for b in ringA + ringD}
        ring_sems_sw = {id(b): P.new_sem("ldsw") for b in ringA + ringD}
        xb = [P.sbuf("xb0", (128, 5, D), F32), P.sbuf("xb1", (128, 4, D), F32)]
        xsem = [P.new_sem("x0"), P.new_sem("x1")]
        hTs = [P.sbuf(f"hT{i}", (128, 8, 640), BF16) for i in range(2)]
        hpre = Ring([P.sbuf(f"hpre{i}", (128, D), BF16) for i in range(5)])
        AT = [P.sbuf(f"AT{i}", (128, 4, 640), BF16) for i in range(2)]
        stmp = Ring([P.sbuf(f"stmp{i}", (128, 512), F32) for i in range(2)])
        junk = P.sbuf("junk", (128, D), BF16)
        stat = P.sbuf("stat", (128, 3, 64), F32)
        gfin = P.sbuf("gfin", (128, D), F32)
        gcols = P.sbuf("gcols", (128, 3, 8), F32)
        epsb = P.sbuf("epsb", (128, 1), F32)
        ob = Ring([P.sbuf(f"ob{i}", (128, D), F32) for i in range(2)])
        obsem = {}
        ident = P.sbuf("ident", (128, 128), BF16)
        ones = P.sbuf("ones", (128, 128), BF16)
        masks = P.sbuf("masks", (128, 3, 512), BF16)
        convw = P.sbuf("convw", (128, 4, 3), F32)
        sinks = P.sbuf("sinks", (128, 4), F32)
        esink = P.sbuf("esink", (128, 4), F32)
        EST = P.sbuf("EST", (128, 4, 128), F32)
        zeros = P.sbuf("zeros", (128, 128), F32)
        cs = P.sbuf("cs", (128, 2, 640), F32)
        cssem = P.new_sem("cs")
        qrot = P.sbuf("qrot", (128, 4, 512), BF16)
        krot = P.sbuf("krot", (128, 640), BF16)
        Vt = P.sbuf("Vt", (128, 5, 128), BF16)
        yA = P.sbuf("yA", (128, 4, 512), BF16)
        yB = P.sbuf("yB", (128, 4, 512), BF16)
        cu = [P.sbuf(f"cu{i}", (128, 514), F32) for i in range(4)]
        ytmp = Ring([P.sbuf(f"ytmp{i}", (128, 512), F32) for i in range(1)])
        csb = Ring([P.sbuf(f"csb{i}", (128, 512), F32) for i in range(2)])
        csbh = P.sbuf("csbh", (128, 2), F32)
        rt1 = Ring([P.sbuf(f"rt1_{i}", (128, 512), F32) for i in range(1)])
        rt2 = Ring([P.sbuf(f"rt2_{i}", (128, 512), F32) for i in range(1)])
        qbf = Ring([P.sbuf(f"qbf{i}", (128, 512), BF16) for i in range(2)])
        permT = P.sbuf("permT", (128, 128), BF16)
        Eb = Ring([P.sbuf(f"E{i}", (128, 512), BF16) for i in range(4)])
        Pb = Ring([P.sbuf(f"P{i}", (128, 512), BF16) for i in range(8)])
        rden = Ring([P.sbuf(f"rden{i}", (128, 512), F32) for i in range(2)])
        psF = [P.psum(f"psF{i}", (128, 512), F32) for i in range(6)]
        psT = [P.psum(f"psT{i}", (128, 8, 128), BF16) for i in range(2)]
        banks = Ring(psF)
        tbanks = Ring(psT)
        csem_c = P.new_sem("const")
        assert P.sbuf_bytes <= 206 * 1024, P.sbuf_bytes

        dma_group(P, sp, [(buf.t[:], src, [], [buf.v()]) for buf, src in (
            (ident, ident_d[:]), (permT, perm_d[:]), (masks, masks_d[:]), (convw, convw_d[:]), (sinks, sinks_d[:]),
            (gcols, gcols_d[:]), (gfin, gfin_d[0:1, :].partition_broadcast(128)))], csem_c)
        P.op(dve, [], [epsb.v()], lambda: nc.vector.memset(epsb.t[:], EPS))
        P.op(dve, [], [ones.v()], lambda: nc.vector.memset(ones.t[:], 1.0))
        P.op(dve, [], [zeros.v()], lambda: nc.vector.memset(zeros.t[:], 0.0))
        P.op(act, [sinks.v()], [esink.v()],
             lambda: nc.scalar.activation(out=esink.t[:], in_=sinks.t[:], func=AF.Exp))
        for c in range(4):
            P.op(dve, [zeros.v(), esink.v()], [EST.v(slice(None), c)],
                 lambda c=c: nc.vector.tensor_scalar(out=EST.t[:, c, :], in0=zeros.t[:, :], scalar1=esink.t[:, c:c + 1],
                                                     scalar2=None, op0=ALU.add))
        for i in range(4):
            P.op(dve, [], [cu[i].v(slice(None), slice(0, 2))], lambda i=i: nc.vector.memset(cu[i].t[:, 0:2], 0.0))

        state = dict(pos=0, outstanding={"A": [], "D": []}, slot={}, nA=0, nD=0)
        total_jobs = [(t, j) for t in range(n_tiles) for j in order]

        def pump(limit_pos):
            while state["pos"] < min(limit_pos, len(total_jobs)):
                t, j = total_jobs[state["pos"]]
                jb = jobs[j]
                ring = ringA if jb["ring"] == "A" else ringD
                outs = state["outstanding"][jb["ring"]]
                if len(outs) >= len(ring):
                    return
                key = "nA" if jb["ring"] == "A" else "nD"
                slot = ring[state[key] % len(ring)]
                state[key] += 1
                shp = jb["shape"]
                dst = slot.v(slice(0, shp[0]), slice(0, shp[1]), slice(0, shp[2]))
                full = View(jb["buf"], ((0, 1),), None)
                late = (order.index(j) % 2 == 1) and n_tiles > 2
                cast = (t == 0) or (t == 1 and late)
                store = (t == 0 and not late and n_tiles > 1) or (t == 1 and late)
                if cast:
                    dma_group(P, pool, [(slot.t[p0:p1, 0:shp[1], 0:shp[2]], src, [], [slot.v()])
                                        for (p0, p1), src in jb["srcs"]], ring_sems_sw[id(slot)])
                    if store:
                        P.dma(sp, jb["scr"][:], dst.ap, jb["csem"], [dst], [full])
                else:
                    P.dma(sp, dst.ap, jb["scr"][:], ring_sems[id(slot)], [full], [slot.v()])
                outs.append((t, j))
                state["slot"][(t, j)] = slot
                state["pos"] += 1

        LOOKAHEAD = 6

        def wget(t, j):
            idx = total_jobs.index((t, j), max(0, state["pos"] - 8))
            pump(idx + 1 + LOOKAHEAD)
            assert (t, j) in state["slot"], ("weight job not loaded (ring full?)", t, j)
            return state["slot"][(t, j)]

        def wrelease(t, j):
            r = jobs[j]["ring"]
            state["outstanding"][r].remove((t, j))
            del state["slot"][(t, j)]

        statn = [0]

        def rms_rinv(x, xs):
            c = statn[0] % 64
            statn[0] += 1
            xv = x.v(slice(None), xs)
            P.op(act, [xv], [junk.v(), stat.v(slice(None), 0, slice(c, c + 1))],
                 lambda: nc.scalar.activation(out=junk.t[:], in_=xv.ap, func=AF.Square,
                                              accum_out=stat.t[:, 0, c:c + 1]))
            P.op(act, [stat.v(slice(None), 0, slice(c, c + 1)), epsb.v()], [stat.v(slice(None), 1, slice(c, c + 1))],
                 lambda: nc.scalar.activation(out=stat.t[:, 1, c:c + 1], in_=stat.t[:, 0, c:c + 1], func=AF.Ln,
                                              scale=1.0 / D, bias=epsb.t[:, 0:1]))
            P.op(act, [stat.v(slice(None), 1, slice(c, c + 1))], [stat.v(slice(None), 2, slice(c, c + 1))],
                 lambda: nc.scalar.activation(out=stat.t[:, 2, c:c + 1], in_=stat.t[:, 1, c:c + 1], func=AF.Exp,
                                              scale=-0.5))
            return c

        def norm_p1(x, xs):
            c = rms_rinv(x, xs)
            xv = x.v(slice(None), xs)
            hp = hpre.next()
            P.op(act, [xv, stat.v(slice(None), 2, slice(c, c + 1))], [hp.v()],
                 lambda: nc.scalar.activation(out=hp.t[:], in_=xv.ap, func=AF.Copy, scale=stat.t[:, 2, c:c + 1]))
            return hp

        def norm_p2(hp, hT, col0, gidx):
            tb = tbanks.next()
            P.tr_group([(tb.v(slice(None), kc), hp.v(slice(None), slice(kc * 128, (kc + 1) * 128)))
                        for kc in range(8)], ident.v(), tb.v())
            hv = hT.v(slice(None), slice(None), slice(col0, col0 + 128))
            gap = gcols.t[:, gidx, :].unsqueeze(2).to_broadcast([128, 8, 128])
            P.op(dve, [tb.v(), gcols.v()], [hv],
                 lambda: nc.vector.tensor_tensor(out=hv.ap, in0=tb.t[:], in1=gap, op=ALU.mult))

        class NormPipe:
            def __init__(self, x, hT, gidx):
                self.x, self.hT, self.gidx = x, hT, gidx
                self.pending = None

            def block(self, xs, col0):
                hp = norm_p1(self.x, xs)
                if self.pending is not None:
                    norm_p2(self.pending[0], self.hT, self.pending[1], self.gidx)
                self.pending = (hp, col0)

            def flush(self):
                if self.pending is not None:
                    norm_p2(self.pending[0], self.hT, self.pending[1], self.gidx)
                self.pending = None

        def ffn(t, k, x, xslots, coltiles, hT, tail_cb=None, tail_flush=None, hooks=None, mid_cb=None):
            lst = ffn_jobs[k]
            hooks = hooks or {}

            def up(gi):
                jg, ju, jd, nch = lst[gi]
                Wg = wget(t, jg)
                Wu = wget(t, ju)
                at = AT[gi % 2]
                for jj in range(nch):
                    cts = coltiles
                    if gi == 0 and jj == 0:
                        cts = []
                        for (c0, c1) in coltiles:
                            if c1 - c0 == 512:
                                cts += [(c0, c0 + 256), (c0 + 256, c1)]
                            else:
                                cts.append((c0, c1))
                    for ci, (c0, c1) in enumerate(cts):
                        if gi == 0 and jj == 0 and mid_cb is not None and ci == len(cts) - 1:
                            mid_cb()
                        n = c1 - c0
                        bg = banks.next()
                        P.mm_group(bg.v(slice(None), slice(0, n)),
                                   [(Wg.v(slice(None), kc, slice(jj * 128, jj * 128 + 128)),
                                     hT.v(slice(None), kc, slice(c0, c1))) for kc in range(8)])
                        bu = banks.next()
                        P.mm_group(bu.v(slice(None), slice(0, n)),
                                   [(Wu.v(slice(None), kc, slice(jj * 128, jj * 128 + 128)),
                                     hT.v(slice(None), kc, slice(c0, c1))) for kc in range(8)])
                        s = stmp.next()
                        bgv = bg.v(slice(None), slice(0, n))
                        buv = bu.v(slice(None), slice(0, n))
                        sv = s.v(slice(None), slice(0, n))
                        P.op(act, [bgv], [sv], lambda: nc.scalar.activation(out=sv.ap, in_=bgv.ap, func=AF.Silu))
                        av = at.v(slice(None), jj, slice(c0, c1))
                        P.op(dve, [sv, buv], [av],
                             lambda: nc.vector.tensor_tensor(out=av.ap, in0=buv.ap, in1=sv.ap, op=ALU.mult))
                wrelease(t, jg)
                wrelease(t, ju)

            def down(gi):
                jg, ju, jd, nch = lst[gi]
                Wd = wget(t, jd)
                at = AT[gi % 2]
                for (xs, col0) in xslots:
                    for dh in range(2):
                        b = banks.next()
                        P.mm_group(b.v(), [(at.v(slice(None), jj, slice(col0, col0 + 128)),
                                            Wd.v(slice(None), jj, slice(dh * 512, dh * 512 + 512))) for jj in range(nch)])
                        xv = x.v(slice(None), xs, slice(dh * 512, dh * 512 + 512))
                        P.op(dve, [b.v(), xv], [xv],
                             lambda: nc.vector.scalar_tensor_tensor(out=xv.ap, in0=b.t[:], scalar=0.5, in1=xv.ap,
                                                                    op0=ALU.mult, op1=ALU.add))
                wrelease(t, jd)

            def down_tail(g0, g1):
                n0, n1 = lst[g0][3], lst[g1][3]
                W0 = wget(t, lst[g0][2])
                W1 = wget(t, lst[g1][2])
                a0, a1 = AT[g0 % 2], AT[g1 % 2]
                for (xs, col0) in xslots:
                    for dh in range(2):
                        b = banks.next()
                        pairs = [(a0.v(slice(None), jj, slice(col0, col0 + 128)),
                                  W0.v(slice(None), jj, slice(dh * 512, dh * 512 + 512))) for jj in range(n0)]
                        pairs += [(a1.v(slice(None), jj, slice(col0, col0 + 128)),
                                   W1.v(slice(None), jj, slice(dh * 512, dh * 512 + 512))) for jj in range(n1)]
                        P.mm_group(b.v(), pairs)
                        xv = x.v(slice(None), xs, slice(dh * 512, dh * 512 + 512))
                        P.op(dve, [b.v(), xv], [xv],
                             lambda: nc.vector.scalar_tensor_tensor(out=xv.ap, in0=b.t[:], scalar=0.5, in1=xv.ap,
                                                                    op0=ALU.mult, op1=ALU.add))
                    if tail_cb is not None:
                        tail_cb(xs, col0)
                if tail_flush is not None:
                    tail_flush()
                wrelease(t, lst[g0][2])
                wrelease(t, lst[g1][2])

            ng = len(lst)
            for gi in range(ng):
                up(gi)
                if gi in hooks:
                    hooks[gi]()
                if 0 < gi < ng - 1:
                    down(gi - 1)
            down_tail(ng - 2, ng - 1)

        def rotary_a(bq, n):
            qb = qbf.next()
            P.op(act, [bq.v(slice(None), slice(0, n))], [qb.v(slice(None), slice(0, n))],
                 lambda: nc.scalar.copy(out=qb.t[:, 0:n], in_=bq.t[:, 0:n]))
            return qb

        def rotary_b(bq, qb, c0, n, outv):
            b2 = banks.next()
            b2v = b2.v(slice(None), slice(0, n))
            P.mm_group(b2v, [(permT.v(), qb.v(slice(None), slice(0, n)))])
            t1 = rt1.next()
            t2 = rt2.next()
            bqv = bq.v(slice(None), slice(0, n))
            cv = cs.v(slice(None), 0, slice(c0, c0 + n))
            sv = cs.v(slice(None), 1, slice(c0, c0 + n))
            t1v = t1.v(slice(None), slice(0, n))
            t2v = t2.v(slice(None), slice(0, n))
            P.op(dve, [bqv, cv, qb.v(slice(None), slice(0, n))], [t1v], lambda: nc.vector.tensor_tensor(out=t1v.ap, in0=bqv.ap, in1=cv.ap, op=ALU.mult))
            P.op(dve, [b2v, sv], [t2v], lambda: nc.vector.tensor_tensor(out=t2v.ap, in0=b2v.ap, in1=sv.ap, op=ALU.mult))
            P.op(dve, [t1v, t2v], [outv], lambda: nc.vector.tensor_tensor(out=outv.ap, in0=t1v.ap, in1=t2v.ap, op=ALU.add))

        def inproj_chunk(W, coff, c0, c1, hT):
            b = banks.next()
            P.mm_group(b.v(slice(None), slice(0, c1 - c0)),
                       [(W.v(slice(None), kc, slice(coff, coff + 128)), hT.v(slice(None), kc, slice(c0, c1)))
                        for kc in range(8)])
            return b

        def mixer(t, x, xo, hT, npipe, pre_cb=None):
            first = (t == 0)
            if not first:
                P.op(pool, [krot.v(slice(None), slice(512, 640))], [krot.v(slice(None), slice(0, 128))],
                     lambda: nc.gpsimd.tensor_copy(out=krot.t[:, 0:128], in_=krot.t[:, 512:640]))
                P.op(pool, [Vt.v(slice(None), 4)], [Vt.v(slice(None), 0)],
                     lambda: nc.gpsimd.tensor_copy(out=Vt.t[:, 0, :], in_=Vt.t[:, 4, :]))
            dma_group(P, sp, [(cs.t[:, 0, :], cos_d[:, t * TILE:t * TILE + 640], [], [cs.v(slice(None), 0)]),
                              (cs.t[:, 1, :], sin_d[:, t * TILE:t * TILE + 640], [], [cs.v(slice(None), 1)])], cssem)
            coltiles = ([(0, 128)] if first else []) + [(128, 640)]
            W = wget(t, mix_jobs[0])
            kpend = None
            for (c0, c1) in ([(0, 128)] if first else []) + [(128, 384), (384, 640)]:
                if c0 == 384 and pre_cb is not None:
                    pre_cb()
                bk = inproj_chunk(W, 0, c0, c1, hT)
                qb = rotary_a(bk, c1 - c0)
                if kpend is not None:
                    rotary_b(*kpend)
                kpend = (bk, qb, c0, c1 - c0, krot.v(slice(None), slice(c0, c1)))
            rotary_b(*kpend)
            vblocks = ([0] if first else []) + [1, 2, 3, 4]
            for vb in vblocks:
                b = banks.next()
                bv = b.v(slice(None), slice(0, 128))
                P.mm_group(bv, [(hT.v(slice(None), kc, slice(vb * 128, vb * 128 + 128)),
                                 W.v(slice(None), kc, slice(128, 256))) for kc in range(8)])
                vv = Vt.v(slice(None), vb)
                P.op(act, [bv], [vv], lambda: nc.scalar.copy(out=vv.ap, in_=bv.ap))
            wrelease(t, mix_jobs[0])
            W = wget(t, mix_jobs[1])
            qpend = None
            for i in range(4):
                bq = inproj_chunk(W, i * 128, 128, 640, hT)
                qb = rotary_a(bq, 512)
                if qpend is not None:
                    rotary_b(*qpend)
                qpend = (bq, qb, 128, 512, qrot.v(slice(None), i))
            rotary_b(*qpend)
            wrelease(t, mix_jobs[1])

            def attn_qk(n):
                res = []
                for g in range(2):
                    pr = slice(64 * g, 64 * g + 64)
                    qv = qrot.v(pr, slice(None), slice(n * 128, n * 128 + 128))
                    pair = []
                    for which in range(2):
                        ks0 = (n + which) * 128
                        b = banks.next()
                        bo = b.v().with_ap(b.t[:, :].rearrange("p (c q) -> p c q", c=4))
                        P.mm_group(bo, [(krot.v(pr, slice(ks0, ks0 + 128)), qv)])
                        e = Eb.next()
                        P.op(act, [b.v()], [e.v()],
                             lambda b=b, e=e: nc.scalar.activation(out=e.t[:], in_=b.t[:], func=AF.Exp, scale=SCALE))
                        pb = Pb.next()
                        mi = (2 if (first and n == 0) else 1) if which == 0 else 0
                        mv = masks.v(slice(None), mi)
                        P.op(dve, [e.v(), mv], [pb.v()],
                             lambda e=e, pb=pb, mv=mv: nc.vector.tensor_tensor(out=pb.t[:], in0=e.t[:], in1=mv.ap,
                                                                               op=ALU.mult))
                        pair.append(pb)
                    res.append(pair)
                return res

            def attn_pv(n, res):
                for g in range(2):
                    pr = slice(64 * g, 64 * g + 64)
                    M = 64 * (g + 1)
                    pp, pc = res[g]
                    bo = banks.next()
                    P.mm_group(bo.v(slice(0, M)), [(Vt.v(slice(None), n, slice(0, M)), pp.v()),
                                                   (Vt.v(slice(None), n + 1, slice(0, M)), pc.v())])
                    bd = banks.next()
                    P.mm_group(bd.v(slice(0, M)), [(ones.v(slice(None), slice(0, M)), pp.v()),
                                                   (ones.v(slice(None), slice(0, M)), pc.v())])
                    r = rden.next()
                    rv = r.v(pr)
                    estv = EST.v(pr)
                    for c in range(4):
                        P.op(act, [bd.v(pr, slice(c * 128, c * 128 + 128)), esink.v()], [r.v(pr, slice(c * 128, c * 128 + 128))],
                             lambda c=c: nc.scalar.activation(out=r.t[pr, c * 128:c * 128 + 128],
                                                              in_=bd.t[pr, c * 128:c * 128 + 128], func=AF.Ln,
                                                              bias=esink.t[pr, c:c + 1], scale=1.0))
                    P.op(act, [rv], [rv],
                         lambda: nc.scalar.activation(out=r.t[pr, :], in_=r.t[pr, :], func=AF.Exp, scale=-1.0))
                    yv = yB.v(pr, slice(None), slice(n * 128, n * 128 + 128))
                    P.op(dve, [bo.v(pr), rv], [yv],
                         lambda: nc.vector.tensor_tensor(out=yv.ap,
                                                         in0=bo.t[pr, :].rearrange("p (c q) -> p c q", c=4),
                                                         in1=r.t[pr, :].rearrange("p (c q) -> p c q", c=4), op=ALU.mult))

            def conv_chunk(i):
                W = wget(t, mix_jobs[2 + i])
                cui = cu[i]
                if first:
                    bch = inproj_chunk(W, 0, 0, 128, hT)
                    buh = inproj_chunk(W, 128, 0, 128, hT)
                    P.op(act, [bch.v(slice(None), slice(126, 128))], [csbh.v()],
                         lambda: nc.scalar.copy(out=csbh.t[:], in_=bch.t[:, 126:128]))
                    P.op(dve, [csbh.v(), buh.v(slice(None), slice(126, 128))], [cui.v(slice(None), slice(0, 2))],
                         lambda: nc.vector.tensor_tensor(out=cui.t[:, 0:2], in0=buh.t[:, 126:128], in1=csbh.t[:],
                                                         op=ALU.mult))
                bc = inproj_chunk(W, 0, 128, 640, hT)
                bu = inproj_chunk(W, 128, 128, 640, hT)
                bb = inproj_chunk(W, 256, 128, 640, hT)
                cb = csb.next()
                P.op(act, [bc.v()], [cb.v()], lambda: nc.scalar.copy(out=cb.t[:], in_=bc.t[:]))
                P.op(dve, [cb.v(), bu.v()], [cui.v(slice(None), slice(2, 514))],
                     lambda: nc.vector.tensor_tensor(out=cui.t[:, 2:514], in0=bu.t[:], in1=cb.t[:], op=ALU.mult))
                y = ytmp.next()
                P.op(dve, [cui.v(slice(None), slice(2, 514)), convw.v()], [y.v()],
                     lambda: nc.vector.tensor_scalar(out=y.t[:], in0=cui.t[:, 2:514], scalar1=convw.t[:, i, 2:3],
                                                     scalar2=None, op0=ALU.mult))
                for (lo, kk) in ((1, 1), (0, 0)):
                    P.op(dve, [cui.v(slice(None), slice(lo, lo + 512)), convw.v(), y.v()], [y.v()],
                         lambda: nc.vector.scalar_tensor_tensor(out=y.t[:], in0=cui.t[:, lo:lo + 512],
                                                                scalar=convw.t[:, i, kk:kk + 1], in1=y.t[:],
                                                                op0=ALU.mult, op1=ALU.add))
                yv = yA.v(slice(None), i)
                P.op(dve, [bb.v(), y.v()], [yv],
                     lambda: nc.vector.tensor_tensor(out=yv.ap, in0=bb.t[:], in1=y.t[:], op=ALU.mult))
                P.op(pool, [cui.v(slice(None), slice(512, 514))], [cui.v(slice(None), slice(0, 2))],
                     lambda: nc.gpsimd.tensor_copy(out=cui.t[:, 0:2], in_=cui.t[:, 512:514]))
                wrelease(t, mix_jobs[2 + i])

            res = attn_qk(0)
            for n in range(4):
                res_next = attn_qk(n + 1) if n + 1 < 4 else None
                conv_chunk(n)
                attn_pv(n, res)
                res = res_next

            Woc = wget(t, job_oc)
            Woa = wget(t, job_oa)
            for tb in range(4):
                for dh in range(2):
                    b = banks.next()
                    pairs = [(yA.v(slice(None), i, slice(tb * 128, tb * 128 + 128)),
                              Woc.v(slice(None), i, slice(dh * 512, dh * 512 + 512))) for i in range(4)]
                    pairs += [(yB.v(slice(None), c, slice(tb * 128, tb * 128 + 128)),
                               Woa.v(slice(None), c, slice(dh * 512, dh * 512 + 512))) for c in range(4)]
                    P.mm_group(b.v(), pairs)
                    xv = x.v(slice(None), xo + tb, slice(dh * 512, dh * 512 + 512))
                    P.op(dve, [b.v(), xv], [xv],
                         lambda: nc.vector.tensor_tensor(out=xv.ap, in0=b.t[:], in1=xv.ap, op=ALU.add))
                npipe.block(xo + tb, 128 + tb * 128)
            wrelease(t, job_oc)
            wrelease(t, job_oa)

        def final_norm_block(t, x, xs, b):
            c = rms_rinv(x, xs)
            xv = x.v(slice(None), xs)
            o = ob.next()
            if id(o) not in obsem:
                obsem[id(o)] = P.new_sem("ob")
            P.op(dve, [xv, stat.v(slice(None), 2, slice(c, c + 1)), gfin.v()], [o.v()],
                 lambda: nc.vector.scalar_tensor_tensor(out=o.t[:], in0=xv.ap, scalar=stat.t[:, 2, c:c + 1],
                                                        in1=gfin.t[:], op0=ALU.mult, op1=ALU.mult))
            r0 = t * TILE + b * 128
            P.dma(sp, out_d[r0:r0 + 128, :], o.t[:], obsem[id(o)], [o.v()], [])

        def load_x(t):
            x = xb[t % 2]
            if t == 0:
                src = xin[0:640, :].rearrange("(b p) d -> p b d", p=128)
                P.dma(sp, x.t[:], src, xsem[0], [], [x.v()])
            else:
                r0 = HALO + t * TILE
                src = xin[r0:r0 + TILE, :].rearrange("(b p) d -> p b d", p=128)
                xo = 1 if t % 2 == 0 else 0
                P.dma(sp, x.t[:, xo:xo + 4, :], src, xsem[t % 2], [], [x.v(slice(None), slice(xo, xo + 4))])

        def tile_geom(t):
            xo = 1 if t % 2 == 0 else 0
            own = [(xo + b, 128 + b * 128) for b in range(4)]
            slots = ([(0, 0)] if t == 0 else []) + own
            coltiles = ([(0, 128)] if t == 0 else []) + [(128, 640)]
            return xo, own, slots, coltiles

        hsel = [0]

        def next_hT():
            h = hTs[hsel[0] % 2]
            hsel[0] += 1
            return h

        load_x(0)
        xo, own, slots, coltiles = tile_geom(0)
        hT1 = next_hT()
        for (xs, col0) in slots:
            norm_p2(norm_p1(xb[0], xs), hT1, col0, 0)
        for t in range(n_tiles):
            x = xb[t % 2]
            xo, own, slots, coltiles = tile_geom(t)
            hTm = next_hT()
            np1 = NormPipe(x, hTm, 1)
            ffn(t, 1, x, slots, coltiles, hT1, tail_cb=np1.block)
            if t + 1 < n_tiles:
                load_x(t + 1)
            hT2 = next_hT()
            np2 = NormPipe(x, hT2, 2)
            mixer(t, x, xo, hTm, np2, pre_cb=np1.flush)
            hooks = {}
            if t + 1 < n_tiles:
                xn = xb[(t + 1) % 2]
                _, _, nslots, _ = tile_geom(t + 1)
                hT1n = next_hT()
                pend = []

                def h_p1(xn=xn, nslots=nslots, pend=pend):
                    for (xs, col0) in nslots:
                        pend.append((norm_p1(xn, xs), col0))

                def h_p2(pend=pend, hT1n=hT1n):
                    for hp, col0 in pend:
                        norm_p2(hp, hT1n, col0, 0)

                hooks = {1: h_p1, 3: h_p2}
            blk = [0]

            def fin_cb(xs, col0, t=t, x=x, blk=blk):
                final_norm_block(t, x, xs, blk[0])
                blk[0] += 1

            ffn(t, 2, x, own, [(128, 640)], hT2, tail_cb=fin_cb, hooks=hooks, mid_cb=np2.flush)
            if t + 1 < n_tiles:
                hT1 = hT1n
        for s in obsem.values():
            nc.sync.wait_ge(s.h, s.count)
    return nc


def _winp_cols():
    B0, C0, U0, Q0, K0, V0 = 0, 512, 1024, 1536, 2048, 2176

    perm = list(range(0, 8)) + list(range(16, 40)) + list(range(8, 16)) + list(range(40, 64))
    cols = []
    cols += [K0 + g * 64 + perm[p] for g in range(2) for p in range(64)]
    cols += [V0 + j for j in range(128)]
    for i in range(4):
        heads = (i, 4 + i)
        cols += [Q0 + h * 64 + perm[p] for h in heads for p in range(64)]
    for i in range(4):
        cols += [C0 + i * 128 + j for j in range(128)]
        cols += [U0 + i * 128 + j for j in range(128)]
        cols += [B0 + i * 128 + j for j in range(128)]
    assert len(cols) == WINP_COLS
    return np.asarray(cols)


def _tables(half):
    pos = (np.arange(HALO + TOK_CORE, dtype=np.int64) + half * TOK_CORE - HALO)
    posf = np.maximum(pos, 0).astype(np.float64)
    inv_freq = np.float64(ROPE_THETA) ** (-(np.arange(0, 16, 2, dtype=np.float64)) / 16.0)
    ang = posf[None, :] * inv_freq[:, None]
    c8 = np.cos(ang).astype(np.float32)
    s8 = np.sin(ang).astype(np.float32)
    cos_t = np.ones((128, HALO + TOK_CORE), np.float32)
    sin_t = np.zeros((128, HALO + TOK_CORE), np.float32)
    for g in range(2):
        cos_t[g * 64:g * 64 + 8] = c8
        cos_t[g * 64 + 32:g * 64 + 40] = c8
        sin_t[g * 64:g * 64 + 8] = -s8
        sin_t[g * 64 + 32:g * 64 + 40] = s8
    return cos_t, sin_t


_NC_CACHE = {}


def kernel(x, ffn1_norm, ffn1_w_gate, ffn1_w_up, ffn1_w_down, mix_norm, w_in, conv_w, attn_sinks, w_out,
           ffn2_norm, ffn2_w_gate, ffn2_w_up, ffn2_w_down, final_norm, _n_tiles=NT_FULL, _cores=None):
    f32 = np.float32
    x = np.asarray(x, f32)
    B, S, _ = x.shape
    c = lambda a: np.ascontiguousarray(np.asarray(a, f32))
    winp = c(np.asarray(w_in, f32)[0][:, _winp_cols()])
    gcols = np.stack([np.asarray(g, f32)[0].reshape(8, 128).T for g in (ffn1_norm, mix_norm, ffn2_norm)], axis=1)
    gcols = c(gcols)
    gfin = c(np.asarray(final_norm, f32).reshape(1, D))
    convw = c(np.asarray(conv_w, f32)[0].T.reshape(4, 128, 3).transpose(1, 0, 2))
    sk = np.asarray(attn_sinks, f32)[0]
    sinks = np.zeros((128, 4), f32)
    sinks[0:64, :] = sk[None, 0:4]
    sinks[64:128, :] = sk[None, 4:8]
    kk = np.arange(128)[:, None]
    qq = np.arange(128)[None, :]
    mC = (kk <= qq).astype(f32)
    mP = (kk > qq).astype(f32)
    ident = np.eye(128, dtype=f32).astype(ml_dtypes.bfloat16)
    permT = np.eye(128, dtype=f32)[np.arange(128) ^ 32].astype(ml_dtypes.bfloat16)
    common = dict(
        wg1=c(ffn1_w_gate[0]), wu1=c(ffn1_w_up[0]), wd1=c(ffn1_w_down[0]),
        wg2=c(ffn2_w_gate[0]), wu2=c(ffn2_w_up[0]), wd2=c(ffn2_w_down[0]),
        winp=winp, wout=c(w_out[0]), gcols=gcols, gfin=gfin, convw=convw, sinks=sinks, ident=ident, permT=permT)
    tabs = [_tables(0), _tables(1)]
    in_maps = []
    cores = list(range(NCORES)) if _cores is None else _cores
    for ci in cores:
        b, half = ci // 2, ci % 2
        xi = np.zeros((HALO + TOK_CORE, D), f32)
        xi[HALO:] = x[b, half * TOK_CORE:(half + 1) * TOK_CORE]
        if half == 1:
            xi[:HALO] = x[b, TOK_CORE - HALO:TOK_CORE]
        mP0 = mP if half == 1 else np.zeros_like(mP)
        masks = np.stack([np.tile(m, (1, 4)) for m in (mC, mP, mP0)], axis=1).astype(ml_dtypes.bfloat16)
        m = dict(common)
        m.update(xin=xi, cos_t=tabs[half][0], sin_t=tabs[half][1], masks=np.ascontiguousarray(masks))
        in_maps.append(m)
    key = _n_tiles
    if key not in _NC_CACHE:
        _NC_CACHE[key] = build_program(_n_tiles)
    nc = _NC_CACHE[key]
    res = run_bass_kernel_spmd(nc, in_maps, core_ids=list(range(len(cores))))
    out = np.zeros((B, S, D), f32)
    for k, ci in enumerate(cores):
        b, half = ci // 2, ci % 2
        out[b, half * TOK_CORE:(half + 1) * TOK_CORE] = np.asarray(res.results[k]["out"], f32)
    return out
```

```python
import contextlib
import numpy as np
import ml_dtypes
import concourse.bass as bass
import concourse.mybir as mybir
from concourse.bass_utils import run_bass_kernel_spmd

F32 = mybir.dt.float32
BF16 = mybir.dt.bfloat16
AF = mybir.ActivationFunctionType
ALU = mybir.AluOpType

D = 1024
DFF = 2816
NCORES = 8
TOK_CORE = 4096
HALO = 128
TILE = 512
NT_FULL = TOK_CORE // TILE
GROUPS = [4, 4, 4, 4, 4, 2]
EPS = 1e-5
SCALE = 0.125
ROPE_THETA = 500000.0
WINP_COLS = 256 + 512 + 4 * 384


class Sem:
    def __init__(self, h):
        self.h = h
        self.count = 0


class Engine:
    def __init__(self, name, inst, sem):
        self.name = name
        self.i = inst
        self.sem = sem
        self.waited = {}


class Rec:
    __slots__ = ("box", "sem", "cnt", "w")

    def __init__(self, box, sem, cnt, w):
        self.box, self.sem, self.cnt, self.w = box, sem, cnt, w


class Buf:
    def __init__(self, name, t, shape):
        self.name = name
        self.t = t
        self.shape = tuple(shape)
        self.recs = []

    def v(self, *idx):
        idx = list(idx) + [slice(None)] * (len(self.shape) - len(idx))
        box = []
        for d, k in zip(self.shape, idx):
            if isinstance(k, int):
                box.append((k, k + 1))
            else:
                a = 0 if k.start is None else k.start
                b = d if k.stop is None else k.stop
                assert 0 <= a < b <= d, (self.name, idx, self.shape)
                box.append((a, b))
        ap = self.t[tuple(idx)] if self.t is not None else None
        return View(self, tuple(box), ap)


class View:
    def __init__(self, buf, box, ap):
        self.buf, self.box, self.ap = buf, box, ap

    def with_ap(self, ap):
        return View(self.buf, self.box, ap)


def _overlap(a, b):
    for (a0, a1), (b0, b1) in zip(a, b):
        if a1 <= b0 or b1 <= a0:
            return False
    return True


def _contains(a, b):
    for (a0, a1), (b0, b1) in zip(a, b):
        if b0 < a0 or b1 > a1:
            return False
    return True


class Prog:
    def __init__(self, nc, es):
        self.nc = nc
        self.es = es
        self.nsem = 0
        self.pe = Engine("pe", nc.tensor, self.new_sem("pe"))
        self.act = Engine("act", nc.scalar, self.new_sem("act"))
        self.dve = Engine("dve", nc.vector, self.new_sem("dve"))
        self.pool = Engine("pool", nc.gpsimd, self.new_sem("pool"))
        self.sp = Engine("sp", nc.sync, self.new_sem("sp"))
        self.sbuf_bytes = 0

    def new_sem(self, name):
        self.nsem += 1
        return Sem(self.es.enter_context(self.nc.semaphore(f"s_{name}_{self.nsem}")))

    def sbuf(self, name, shape, dt):
        t = self.es.enter_context(self.nc.sbuf_tensor("sb_" + name, list(shape), dt))
        n = 1
        for s in shape[1:]:
            n *= s
        self.sbuf_bytes += n * (4 if dt == F32 else 2)
        return Buf(name, t, shape)

    def psum(self, name, shape, dt):
        t = self.es.enter_context(self.nc.psum_tensor("ps_" + name, list(shape), dt))
        b = Buf(name, t, shape)
        b.is_psum = True
        return b

    def _sync(self, eng, reads, writes):
        need = {}
        for v in reads:
            psum = getattr(v.buf, "is_psum", False)
            for r in v.buf.recs:
                if (r.w and (psum or _overlap(r.box, v.box))) or (psum and (not r.w) and r.sem is not eng.sem):
                    if need.get(r.sem, 0) < r.cnt:
                        need[r.sem] = r.cnt
        for v in writes:
            psum = getattr(v.buf, "is_psum", False)
            for r in v.buf.recs:
                if psum or _overlap(r.box, v.box):
                    if need.get(r.sem, 0) < r.cnt:
                        need[r.sem] = r.cnt
        for s, c in need.items():
            if s is eng.sem and eng.name == "pe":
                continue
            if eng.waited.get(s, 0) >= c:
                continue
            eng.i.wait_ge(s.h, c)
            eng.waited[s] = c

    def _record(self, reads, writes, sem, cnt):
        for v in reads:
            recs = v.buf.recs
            recs[:] = [r for r in recs if not ((not r.w) and r.sem is sem and r.box == v.box)]
            recs.append(Rec(v.box, sem, cnt, False))
        for v in writes:
            recs = v.buf.recs
            recs[:] = [r for r in recs if not _contains(v.box, r.box)]
            recs.append(Rec(v.box, sem, cnt, True))

    def op(self, eng, reads, writes, fn):
        self._sync(eng, reads, writes)
        ins = fn()
        eng.sem.count += 1
        ins.then_inc(eng.sem.h, 1)
        self._record(reads, writes, eng.sem, eng.sem.count)

    def mm_group(self, out, pairs, transpose_ident=None):
        eng = self.pe
        reads = []
        for a, b in pairs:
            reads += [a, b]
        if transpose_ident is not None:
            reads.append(transpose_ident)
        self._sync(eng, reads, [out])
        n = len(pairs)
        ins = None
        for i, (a, b) in enumerate(pairs):
            ins = self.nc.tensor.matmul(out.ap, a.ap, b.ap, start=(i == 0), stop=(i == n - 1))
        eng.sem.count += 1
        ins.then_inc(eng.sem.h, 1)
        self._record(reads, [out], eng.sem, eng.sem.count)

    def tr_group(self, outs_ins, ident, outbox):
        eng = self.pe
        reads = [i for _, i in outs_ins] + [ident]
        self._sync(eng, reads, [outbox])
        ins = None
        for o, i in outs_ins:
            ins = self.nc.tensor.transpose(o.ap, i.ap, ident.ap)
        eng.sem.count += 1
        ins.then_inc(eng.sem.h, 1)
        self._record(reads, [outbox], eng.sem, eng.sem.count)

    def dma(self, qeng, out_ap, in_ap, sem, reads, writes):
        self._sync(qeng, reads, writes)
        ins = qeng.i.dma_start(out=out_ap, in_=in_ap)
        sem.count += 16
        ins.then_inc(sem.h, 16)
        self._record(reads, writes, sem, sem.count)


def dma_group(P, qeng, items, sem):
    for (out_ap, in_ap, reads, writes) in items:
        P._sync(qeng, reads, writes)
    for (out_ap, in_ap, reads, writes) in items:
        ins = qeng.i.dma_start(out=out_ap, in_=in_ap)
        sem.count += 16
        ins.then_inc(sem.h, 16)
    for (out_ap, in_ap, reads, writes) in items:
        P._record(reads, writes, sem, sem.count)


class Ring:
    def __init__(self, bufs):
        self.bufs = bufs
        self.n = 0

    def next(self):
        b = self.bufs[self.n % len(self.bufs)]
        self.n += 1
        return b


def build_program(n_tiles=NT_FULL):
    nc = bass.Bass("TRN2", target_bir_lowering=False)

    def din(name, shape, dt=F32):
        return nc.dram_tensor(name, list(shape), dt, kind="ExternalInput").ap()

    xin = din("xin", [HALO + TOK_CORE, D])
    w_ffn = {}
    for k in (1, 2):
        w_ffn[k] = (din(f"wg{k}", [D, DFF]), din(f"wu{k}", [D, DFF]), din(f"wd{k}", [DFF, D]))
    winp = din("winp", [D, WINP_COLS])
    wout = din("wout", [D, D])
    gcols_d = din("gcols", [128, 3, 8])
    gfin_d = din("gfin", [1, D])
    convw_d = din("convw", [128, 4, 3])
    sinks_d = din("sinks", [128, 4])
    cos_d = din("cos_t", [128, HALO + TOK_CORE])
    sin_d = din("sin_t", [128, HALO + TOK_CORE])
    masks_d = din("masks", [128, 3, 512], BF16)
    ident_d = din("ident", [128, 128], BF16)
    perm_d = din("permT", [128, 128], BF16)
    out_d = nc.dram_tensor("out", [TOK_CORE, D], F32, kind="ExternalOutput").ap()

    with contextlib.ExitStack() as es:
        P = Prog(nc, es)
        pe, act, dve, pool, sp = P.pe, P.act, P.dve, P.pool, P.sp

        jobs = []

        def add_job(name, ring, shape, srcs):
            scr = nc.dram_tensor(f"scr_{name}", list(shape), BF16, kind="Internal").ap()
            jobs.append(dict(name=name, ring=ring, shape=shape, srcs=srcs, scr=scr,
                             buf=Buf(f"scr_{name}", None, (1,))))
            return len(jobs) - 1

        ffn_jobs = {}
        for k in (1, 2):
            wg, wu, wd = w_ffn[k]
            wgv = wg.rearrange("(kc p) f -> p kc f", p=128)
            wuv = wu.rearrange("(kc p) f -> p kc f", p=128)
            wdv = wd.rearrange("(j p) d -> p j d", p=128)
            lst = []
            c0 = 0
            for gi, nch in enumerate(GROUPS):
                cols = nch * 128
                jg = add_job(f"g{k}_{gi}", "A", (128, 8, cols), [((0, 128), wgv[:, :, c0 * 128:c0 * 128 + cols])])
                ju = add_job(f"u{k}_{gi}", "A", (128, 8, cols), [((0, 128), wuv[:, :, c0 * 128:c0 * 128 + cols])])
                jd = add_job(f"d{k}_{gi}", "D", (128, nch, D), [((0, 128), wdv[:, c0:c0 + nch, :])])
                lst.append((jg, ju, jd, nch))
                c0 += nch
            ffn_jobs[k] = lst
        winv = winp.rearrange("(kc p) f -> p kc f", p=128)
        mix_jobs = []
        mcols = [256, 512, 384, 384, 384, 384]
        c0 = 0
        for mi, cols in enumerate(mcols):
            mix_jobs.append(add_job(f"m{mi}", "A", (128, 8, cols), [((0, 128), winv[:, :, c0:c0 + cols])]))
            c0 += cols
        assert c0 == WINP_COLS
        woc = wout[0:512, :].rearrange("(j p) d -> p j d", p=128)
        job_oc = add_job("oc", "D", (128, 4, D), [((0, 128), woc)])
        wo_a0 = wout[512:768, :].rearrange("(c p) d -> p c d", p=64)
        wo_a1 = wout[768:1024, :].rearrange("(c p) d -> p c d", p=64)
        job_oa = add_job("oa", "D", (128, 4, D), [((0, 64), wo_a0), ((64, 128), wo_a1)])

        order = []
        for k in (1, 2):
            lst = ffn_jobs[k]
            seq = []
            for gi, (jg, ju, jd, nch) in enumerate(lst):
                seq += [jg, ju]
                if gi > 0:
                    seq.append(lst[gi - 1][2])
            seq.append(lst[-1][2])
            if k == 1:
                order += seq + mix_jobs + [job_oc, job_oa]
            else:
                order += seq
        for j in order:
            jobs[j]["csem"] = P.new_sem(f"cv{j}")

        ringA = [P.sbuf(f"rA{i}", (128, 8, 512), BF16) for i in range(4)]
        ringD = [P.sbuf(f"rD{i}", (128, 4, D), BF16) for i in range(2)]
        ring_sems = {id(b): P.new_sem("ld") for b in ringA + ringD}
        ring_sems_sw = {id(b): P.new_sem("ldsw") for b in ringA + ringD}
        xb = [P.sbuf("xb0", (128, 5, D), F32), P.sbuf("xb1", (128, 4, D), F32)]
        xsem = [P.new_sem("x0"), P.new_sem("x1")]
        hTs = [P.sbuf(f"hT{i}", (128, 8, 640), BF16) for i in range(2)]
        hpre = Ring([P.sbuf(f"hpre{i}", (128, D), BF16) for i in range(5)])
        AT = [P.sbuf(f"AT{i}", (128, 4, 640), BF16) for i in range(2)]
        stmp = Ring([P.sbuf(f"stmp{i}", (128, 512), F32) for i in range(2)])
        junk = P.sbuf("junk", (128, D), BF16)
        stat = P.sbuf("stat", (128, 3, 64), F32)
        gfin = P.sbuf("gfin", (128, D), F32)
        gcols = P.sbuf("gcols", (128, 3, 8), F32)
        epsb = P.sbuf("epsb", (128, 1), F32)
        ob = Ring([P.sbuf(f"ob{i}", (128, D), F32) for i in range(2)])
        obsem = {}
        ident = P.sbuf("ident", (128, 128), BF16)
        ones = P.sbuf("ones", (128, 128), BF16)
        masks = P.sbuf("masks", (128, 3, 512), BF16)
        convw = P.sbuf("convw", (128, 4, 3), F32)
        sinks = P.sbuf("sinks", (128, 4), F32)
        esink = P.sbuf("esink", (128, 4), F32)
        EST = P.sbuf("EST", (128, 4, 128), F32)
        zeros = P.sbuf("zeros", (128, 128), F32)
        cs = P.sbuf("cs", (128, 2, 640), F32)
        cssem = P.new_sem("cs")
        qrot = P.sbuf("qrot", (128, 4, 512), BF16)
        krot = P.sbuf("krot", (128, 640), BF16)
        Vt = P.sbuf("Vt", (128, 5, 128), BF16)
        yA = P.sbuf("yA", (128, 4, 512), BF16)
        yB = P.sbuf("yB", (128, 4, 512), BF16)
        cu = [P.sbuf(f"cu{i}", (128, 514), F32) for i in range(4)]
        ytmp = Ring([P.sbuf(f"ytmp{i}", (128, 512), F32) for i in range(1)])
        csb = Ring([P.sbuf(f"csb{i}", (128, 512), F32) for i in range(2)])
        csbh = P.sbuf("csbh", (128, 2), F32)
        rt1 = Ring([P.sbuf(f"rt1_{i}", (128, 512), F32) for i in range(1)])
        rt2 = Ring([P.sbuf(f"rt2_{i}", (128, 512), F32) for i in range(1)])
        qbf = Ring([P.sbuf(f"qbf{i}", (128, 512), BF16) for i in range(2)])
        permT = P.sbuf("permT", (128, 128), BF16)
        Eb = Ring([P.sbuf(f"E{i}", (128, 512), BF16) for i in range(4)])
        Pb = Ring([P.sbuf(f"P{i}", (128, 512), BF16) for i in range(8)])
        rden = Ring([P.sbuf(f"rden{i}", (128, 512), F32) for i in range(2)])
        psF = [P.psum(f"psF{i}", (128, 512), F32) for i in range(6)]
        psT = [P.psum(f"psT{i}", (128, 8, 128), BF16) for i in range(2)]
        banks = Ring(psF)
        tbanks = Ring(psT)
        csem_c = P.new_sem("const")
        assert P.sbuf_bytes <= 206 * 1024, P.sbuf_bytes

        for b0 in range(5):
            P.dma(sp, xb[0].t[:, b0, :], xin[b0 * 128:(b0 + 1) * 128, :], P.new_sem(f"x0b{b0}"), [],
                  [xb[0].v(slice(None), b0)])
        dma_group(P, sp, [(buf.t[:], src, [], [buf.v()]) for buf, src in (
            (ident, ident_d[:]), (permT, perm_d[:]), (masks, masks_d[:]), (convw, convw_d[:]), (sinks, sinks_d[:]),
            (gcols, gcols_d[:]), (gfin, gfin_d[0:1, :].partition_broadcast(128)))], csem_c)
        P.op(dve, [], [epsb.v()], lambda: nc.vector.memset(epsb.t[:], EPS))
        P.op(dve, [], [ones.v()], lambda: nc.vector.memset(ones.t[:], 1.0))
        P.op(dve, [], [zeros.v()], lambda: nc.vector.memset(zeros.t[:], 0.0))
        P.op(act, [sinks.v()], [esink.v()],
             lambda: nc.scalar.activation(out=esink.t[:], in_=sinks.t[:], func=AF.Exp))
        for c in range(4):
            P.op(dve, [zeros.v(), esink.v()], [EST.v(slice(None), c)],
                 lambda c=c: nc.vector.tensor_scalar(out=EST.t[:, c, :], in0=zeros.t[:, :], scalar1=esink.t[:, c:c + 1],
                                                     scalar2=None, op0=ALU.add))
        for i in range(4):
            P.op(dve, [], [cu[i].v(slice(None), slice(0, 2))], lambda i=i: nc.vector.memset(cu[i].t[:, 0:2], 0.0))

        state = dict(pos=0, outstanding={"A": [], "D": []}, slot={}, nA=0, nD=0)
        total_jobs = [(t, j) for t in range(n_tiles) for j in order]

        def pump(limit_pos):
            while state["pos"] < min(limit_pos, len(total_jobs)):
                t, j = total_jobs[state["pos"]]
                jb = jobs[j]
                ring = ringA if jb["ring"] == "A" else ringD
                outs = state["outstanding"][jb["ring"]]
                if len(outs) >= len(ring):
                    return
                key = "nA" if jb["ring"] == "A" else "nD"
                slot = ring[state[key] % len(ring)]
                state[key] += 1
                shp = jb["shape"]
                dst = slot.v(slice(0, shp[0]), slice(0, shp[1]), slice(0, shp[2]))
                full = View(jb["buf"], ((0, 1),), None)
                late = (order.index(j) % 2 == 1) and n_tiles > 2
                cast = (t == 0) or (t == 1 and late)
                store = (t == 0 and not late and n_tiles > 1) or (t == 1 and late)
                if cast:
                    dma_group(P, pool, [(slot.t[p0:p1, 0:shp[1], 0:shp[2]], src, [], [slot.v()])
                                        for (p0, p1), src in jb["srcs"]], ring_sems_sw[id(slot)])
                    if store:
                        P.dma(sp, jb["scr"][:], dst.ap, jb["csem"], [dst], [full])
                else:
                    P.dma(sp, dst.ap, jb["scr"][:], ring_sems[id(slot)], [full], [slot.v()])
                outs.append((t, j))
                state["slot"][(t, j)] = slot
                state["pos"] += 1

        LOOKAHEAD = 6

        def wget(t, j):
            idx = total_jobs.index((t, j), max(0, state["pos"] - 8))
            pump(idx + 1 + LOOKAHEAD)
            assert (t, j) in state["slot"], ("weight job not loaded (ring full?)", t, j)
            return state["slot"][(t, j)]

        def wrelease(t, j):
            r = jobs[j]["ring"]
            state["outstanding"][r].remove((t, j))
            del state["slot"][(t, j)]

        statn = [0]

        def rms_rinv(x, xs):
            c = statn[0] % 64
            statn[0] += 1
            xv = x.v(slice(None), xs)
            P.op(act, [xv], [junk.v(), stat.v(slice(None), 0, slice(c, c + 1))],
                 lambda: nc.scalar.activation(out=junk.t[:], in_=xv.ap, func=AF.Square,
                                              accum_out=stat.t[:, 0, c:c + 1]))
            P.op(act, [stat.v(slice(None), 0, slice(c, c + 1)), epsb.v()], [stat.v(slice(None), 1, slice(c, c + 1))],
                 lambda: nc.scalar.activation(out=stat.t[:, 1, c:c + 1], in_=stat.t[:, 0, c:c + 1], func=AF.Ln,
                                              scale=1.0 / D, bias=epsb.t[:, 0:1]))
            P.op(act, [stat.v(slice(None), 1, slice(c, c + 1))], [stat.v(slice(None), 2, slice(c, c + 1))],
                 lambda: nc.scalar.activation(out=stat.t[:, 2, c:c + 1], in_=stat.t[:, 1, c:c + 1], func=AF.Exp,
                                              scale=-0.5))
            return c

        def norm_p1(x, xs):
            c = rms_rinv(x, xs)
            xv = x.v(slice(None), xs)
            hp = hpre.next()
            P.op(act, [xv, stat.v(slice(None), 2, slice(c, c + 1))], [hp.v()],
                 lambda: nc.scalar.activation(out=hp.t[:], in_=xv.ap, func=AF.Copy, scale=stat.t[:, 2, c:c + 1]))
            return hp

        def norm_p2(hp, hT, col0, gidx):
            tb = tbanks.next()
            P.tr_group([(tb.v(slice(None), kc), hp.v(slice(None), slice(kc * 128, (kc + 1) * 128)))
                        for kc in range(8)], ident.v(), tb.v())
            hv = hT.v(slice(None), slice(None), slice(col0, col0 + 128))
            gap = gcols.t[:, gidx, :].unsqueeze(2).to_broadcast([128, 8, 128])
            P.op(dve, [tb.v(), gcols.v()], [hv],
                 lambda: nc.vector.tensor_tensor(out=hv.ap, in0=tb.t[:], in1=gap, op=ALU.mult))

        class NormPipe:
            def __init__(self, x, hT, gidx):
                self.x, self.hT, self.gidx = x, hT, gidx
                self.pending = None

            def block(self, xs, col0):
                hp = norm_p1(self.x, xs)
                if self.pending is not None:
                    norm_p2(self.pending[0], self.hT, self.pending[1], self.gidx)
                self.pending = (hp, col0)

            def flush(self):
                if self.pending is not None:
                    norm_p2(self.pending[0], self.hT, self.pending[1], self.gidx)
                self.pending = None

        def ffn(t, k, x, xslots, coltiles, hT, tail_cb=None, tail_flush=None, hooks=None, mid_cb=None):
            lst = ffn_jobs[k]
            hooks = hooks or {}

            def up(gi):
                jg, ju, jd, nch = lst[gi]
                Wg = wget(t, jg)
                Wu = wget(t, ju)
                at = AT[gi % 2]
                for jj in range(nch):
                    cts = coltiles
                    if gi == 0 and jj == 0:
                        cts = []
                        for (c0, c1) in coltiles:
                            if c1 - c0 == 512:
                                cts += [(c0, c0 + 256), (c0 + 256, c1)]
                            else:
                                cts.append((c0, c1))
                    for ci, (c0, c1) in enumerate(cts):
                        if gi == 0 and jj == 0 and mid_cb is not None and ci == len(cts) - 1:
                            mid_cb()
                        n = c1 - c0
                        bg = banks.next()
                        P.mm_group(bg.v(slice(None), slice(0, n)),
                                   [(Wg.v(slice(None), kc, slice(jj * 128, jj * 128 + 128)),
                                     hT.v(slice(None), kc, slice(c0, c1))) for kc in range(8)])
                        bu = banks.next()
                        P.mm_group(bu.v(slice(None), slice(0, n)),
                                   [(Wu.v(slice(None), kc, slice(jj * 128, jj * 128 + 128)),
                                     hT.v(slice(None), kc, slice(c0, c1))) for kc in range(8)])
                        s = stmp.next()
                        bgv = bg.v(slice(None), slice(0, n))
                        buv = bu.v(slice(None), slice(0, n))
                        sv = s.v(slice(None), slice(0, n))
                        P.op(act, [bgv], [sv], lambda: nc.scalar.activation(out=sv.ap, in_=bgv.ap, func=AF.Silu))
                        av = at.v(slice(None), jj, slice(c0, c1))
                        P.op(dve, [sv, buv], [av],
                             lambda: nc.vector.tensor_tensor(out=av.ap, in0=buv.ap, in1=sv.ap, op=ALU.mult))
                wrelease(t, jg)
                wrelease(t, ju)

            def down(gi):
                jg, ju, jd, nch = lst[gi]
                Wd = wget(t, jd)
                at = AT[gi % 2]
                for (xs, col0) in xslots:
                    for dh in range(2):
                        b = banks.next()
                        P.mm_group(b.v(), [(at.v(slice(None), jj, slice(col0, col0 + 128)),
                                            Wd.v(slice(None), jj, slice(dh * 512, dh * 512 + 512))) for jj in range(nch)])
                        xv = x.v(slice(None), xs, slice(dh * 512, dh * 512 + 512))
                        P.op(dve, [b.v(), xv], [xv],
                             lambda: nc.vector.scalar_tensor_tensor(out=xv.ap, in0=b.t[:], scalar=0.5, in1=xv.ap,
                                                                    op0=ALU.mult, op1=ALU.add))
                wrelease(t, jd)

            def down_tail(g0, g1):
                n0, n1 = lst[g0][3], lst[g1][3]
                W0 = wget(t, lst[g0][2])
                W1 = wget(t, lst[g1][2])
                a0, a1 = AT[g0 % 2], AT[g1 % 2]
                for (xs, col0) in xslots:
                    for dh in range(2):
                        b = banks.next()
                        pairs = [(a0.v(slice(None), jj, slice(col0, col0 + 128)),
                                  W0.v(slice(None), jj, slice(dh * 512, dh * 512 + 512))) for jj in range(n0)]
                        pairs += [(a1.v(slice(None), jj, slice(col0, col0 + 128)),
                                   W1.v(slice(None), jj, slice(dh * 512, dh * 512 + 512))) for jj in range(n1)]
                        P.mm_group(b.v(), pairs)
                        xv = x.v(slice(None), xs, slice(dh * 512, dh * 512 + 512))
                        P.op(dve, [b.v(), xv], [xv],
                             lambda: nc.vector.scalar_tensor_tensor(out=xv.ap, in0=b.t[:], scalar=0.5, in1=xv.ap,
                                                                    op0=ALU.mult, op1=ALU.add))
                    if tail_cb is not None:
                        tail_cb(xs, col0)
                if tail_flush is not None:
                    tail_flush()
                wrelease(t, lst[g0][2])
                wrelease(t, lst[g1][2])

            ng = len(lst)
            for gi in range(ng):
                up(gi)
                if gi in hooks:
                    hooks[gi]()
                if 0 < gi < ng - 1:
                    down(gi - 1)
            down_tail(ng - 2, ng - 1)

        def rotary_a(bq, n):
            qb = qbf.next()
            P.op(act, [bq.v(slice(None), slice(0, n))], [qb.v(slice(None), slice(0, n))],
                 lambda: nc.scalar.copy(out=qb.t[:, 0:n], in_=bq.t[:, 0:n]))
            return qb

        def rotary_b(bq, qb, c0, n, outv):
            b2 = banks.next()
            b2v = b2.v(slice(None), slice(0, n))
            P.mm_group(b2v, [(permT.v(), qb.v(slice(None), slice(0, n)))])
            t1 = rt1.next()
            t2 = rt2.next()
            bqv = bq.v(slice(None), slice(0, n))
            cv = cs.v(slice(None), 0, slice(c0, c0 + n))
            sv = cs.v(slice(None), 1, slice(c0, c0 + n))
            t1v = t1.v(slice(None), slice(0, n))
            t2v = t2.v(slice(None), slice(0, n))
            P.op(dve, [bqv, cv, qb.v(slice(None), slice(0, n))], [t1v], lambda: nc.vector.tensor_tensor(out=t1v.ap, in0=bqv.ap, in1=cv.ap, op=ALU.mult))
            P.op(dve, [b2v, sv], [t2v], lambda: nc.vector.tensor_tensor(out=t2v.ap, in0=b2v.ap, in1=sv.ap, op=ALU.mult))
            P.op(dve, [t1v, t2v], [outv], lambda: nc.vector.tensor_tensor(out=outv.ap, in0=t1v.ap, in1=t2v.ap, op=ALU.add))

        def inproj_chunk(W, coff, c0, c1, hT):
            b = banks.next()
            P.mm_group(b.v(slice(None), slice(0, c1 - c0)),
                       [(W.v(slice(None), kc, slice(coff, coff + 128)), hT.v(slice(None), kc, slice(c0, c1)))
                        for kc in range(8)])
            return b

        def mixer(t, x, xo, hT, npipe, pre_cb=None):
            first = (t == 0)
            if not first:
                P.op(pool, [krot.v(slice(None), slice(512, 640))], [krot.v(slice(None), slice(0, 128))],
                     lambda: nc.gpsimd.tensor_copy(out=krot.t[:, 0:128], in_=krot.t[:, 512:640]))
                P.op(pool, [Vt.v(slice(None), 4)], [Vt.v(slice(None), 0)],
                     lambda: nc.gpsimd.tensor_copy(out=Vt.t[:, 0, :], in_=Vt.t[:, 4, :]))
            dma_group(P, sp, [(cs.t[:, 0, :], cos_d[:, t * TILE:t * TILE + 640], [], [cs.v(slice(None), 0)]),
                              (cs.t[:, 1, :], sin_d[:, t * TILE:t * TILE + 640], [], [cs.v(slice(None), 1)])], cssem)
            coltiles = ([(0, 128)] if first else []) + [(128, 640)]
            W = wget(t, mix_jobs[0])
            kpend = None
            for (c0, c1) in ([(0, 128)] if first else []) + [(128, 384), (384, 640)]:
                if c0 == 384 and pre_cb is not None:
                    pre_cb()
                bk = inproj_chunk(W, 0, c0, c1, hT)
                qb = rotary_a(bk, c1 - c0)
                if kpend is not None:
                    rotary_b(*kpend)
                kpend = (bk, qb, c0, c1 - c0, krot.v(slice(None), slice(c0, c1)))
            rotary_b(*kpend)
            vblocks = ([0] if first else []) + [1, 2, 3, 4]
            for vb in vblocks:
                b = banks.next()
                bv = b.v(slice(None), slice(0, 128))
                P.mm_group(bv, [(hT.v(slice(None), kc, slice(vb * 128, vb * 128 + 128)),
                                 W.v(slice(None), kc, slice(128, 256))) for kc in range(8)])
                vv = Vt.v(slice(None), vb)
                P.op(act, [bv], [vv], lambda: nc.scalar.copy(out=vv.ap, in_=bv.ap))
            wrelease(t, mix_jobs[0])
            W = wget(t, mix_jobs[1])
            qpend = None
            for i in range(4):
                bq = inproj_chunk(W, i * 128, 128, 640, hT)
                qb = rotary_a(bq, 512)
                if qpend is not None:
                    rotary_b(*qpend)
                qpend = (bq, qb, 128, 512, qrot.v(slice(None), i))
            rotary_b(*qpend)
            wrelease(t, mix_jobs[1])

            def attn_qk(n):
                res = []
                for g in range(2):
                    pr = slice(64 * g, 64 * g + 64)
                    qv = qrot.v(pr, slice(None), slice(n * 128, n * 128 + 128))
                    pair = []
                    for which in range(2):
                        ks0 = (n + which) * 128
                        b = banks.next()
                        bo = b.v().with_ap(b.t[:, :].rearrange("p (c q) -> p c q", c=4))
                        P.mm_group(bo, [(krot.v(pr, slice(ks0, ks0 + 128)), qv)])
                        e = Eb.next()
                        P.op(act, [b.v()], [e.v()],
                             lambda b=b, e=e: nc.scalar.activation(out=e.t[:], in_=b.t[:], func=AF.Exp, scale=SCALE))
                        pb = Pb.next()
                        mi = (2 if (first and n == 0) else 1) if which == 0 else 0
                        mv = masks.v(slice(None), mi)
                        P.op(dve, [e.v(), mv], [pb.v()],
                             lambda e=e, pb=pb, mv=mv: nc.vector.tensor_tensor(out=pb.t[:], in0=e.t[:], in1=mv.ap,
                                                                               op=ALU.mult))
                        pair.append(pb)
                    res.append(pair)
                return res

            def attn_pv(n, res):
                for g in range(2):
                    pr = slice(64 * g, 64 * g + 64)
                    M = 64 * (g + 1)
                    pp, pc = res[g]
                    bo = banks.next()
                    P.mm_group(bo.v(slice(0, M)), [(Vt.v(slice(None), n, slice(0, M)), pp.v()),
                                                   (Vt.v(slice(None), n + 1, slice(0, M)), pc.v())])
                    bd = banks.next()
                    P.mm_group(bd.v(slice(0, M)), [(ones.v(slice(None), slice(0, M)), pp.v()),
                                                   (ones.v(slice(None), slice(0, M)), pc.v())])
                    r = rden.next()
                    rv = r.v(pr)
                    estv = EST.v(pr)
                    for c in range(4):
                        P.op(act, [bd.v(pr, slice(c * 128, c * 128 + 128)), esink.v()], [r.v(pr, slice(c * 128, c * 128 + 128))],
                             lambda c=c: nc.scalar.activation(out=r.t[pr, c * 128:c * 128 + 128],
                                                              in_=bd.t[pr, c * 128:c * 128 + 128], func=AF.Ln,
                                                              bias=esink.t[pr, c:c + 1], scale=1.0))
                    P.op(act, [rv], [rv],
                         lambda: nc.scalar.activation(out=r.t[pr, :], in_=r.t[pr, :], func=AF.Exp, scale=-1.0))
                    yv = yB.v(pr, slice(None), slice(n * 128, n * 128 + 128))
                    P.op(dve, [bo.v(pr), rv], [yv],
                         lambda: nc.vector.tensor_tensor(out=yv.ap,
                                                         in0=bo.t[pr, :].rearrange("p (c q) -> p c q", c=4),
                                                         in1=r.t[pr, :].rearrange("p (c q) -> p c q", c=4), op=ALU.mult))

            def conv_chunk(i):
                W = wget(t, mix_jobs[2 + i])
                cui = cu[i]
                if first:
                    bch = inproj_chunk(W, 0, 0, 128, hT)
                    buh = inproj_chunk(W, 128, 0, 128, hT)
                    P.op(act, [bch.v(slice(None), slice(126, 128))], [csbh.v()],
                         lambda: nc.scalar.copy(out=csbh.t[:], in_=bch.t[:, 126:128]))
                    P.op(dve, [csbh.v(), buh.v(slice(None), slice(126, 128))], [cui.v(slice(None), slice(0, 2))],
                         lambda: nc.vector.tensor_tensor(out=cui.t[:, 0:2], in0=buh.t[:, 126:128], in1=csbh.t[:],
                                                         op=ALU.mult))
                bc = inproj_chunk(W, 0, 128, 640, hT)
                bu = inproj_chunk(W, 128, 128, 640, hT)
                bb = inproj_chunk(W, 256, 128, 640, hT)
                cb = csb.next()
                P.op(act, [bc.v()], [cb.v()], lambda: nc.scalar.copy(out=cb.t[:], in_=bc.t[:]))
                P.op(dve, [cb.v(), bu.v()], [cui.v(slice(None), slice(2, 514))],
                     lambda: nc.vector.tensor_tensor(out=cui.t[:, 2:514], in0=bu.t[:], in1=cb.t[:], op=ALU.mult))
                y = ytmp.next()
                P.op(dve, [cui.v(slice(None), slice(2, 514)), convw.v()], [y.v()],
                     lambda: nc.vector.tensor_scalar(out=y.t[:], in0=cui.t[:, 2:514], scalar1=convw.t[:, i, 2:3],
                                                     scalar2=None, op0=ALU.mult))
                for (lo, kk) in ((1, 1), (0, 0)):
                    P.op(dve, [cui.v(slice(None), slice(lo, lo + 512)), convw.v(), y.v()], [y.v()],
                         lambda: nc.vector.scalar_tensor_tensor(out=y.t[:], in0=cui.t[:, lo:lo + 512],
                                                                scalar=convw.t[:, i, kk:kk + 1], in1=y.t[:],
                                                                op0=ALU.mult, op1=ALU.add))
                yv = yA.v(slice(None), i)
                P.op(dve, [bb.v(), y.v()], [yv],
                     lambda: nc.vector.tensor_tensor(out=yv.ap, in0=bb.t[:], in1=y.t[:], op=ALU.mult))
                P.op(pool, [cui.v(slice(None), slice(512, 514))], [cui.v(slice(None), slice(0, 2))],
                     lambda: nc.gpsimd.tensor_copy(out=cui.t[:, 0:2], in_=cui.t[:, 512:514]))
                wrelease(t, mix_jobs[2 + i])

            res = attn_qk(0)
            for n in range(4):
                res_next = attn_qk(n + 1) if n + 1 < 4 else None
                conv_chunk(n)
                attn_pv(n, res)
                res = res_next

            Woc = wget(t, job_oc)
            Woa = wget(t, job_oa)
            for tb in range(4):
                for dh in range(2):
                    b = banks.next()
                    pairs = [(yA.v(slice(None), i, slice(tb * 128, tb * 128 + 128)),
                              Woc.v(slice(None), i, slice(dh * 512, dh * 512 + 512))) for i in range(4)]
                    pairs += [(yB.v(slice(None), c, slice(tb * 128, tb * 128 + 128)),
                               Woa.v(slice(None), c, slice(dh * 512, dh * 512 + 512))) for c in range(4)]
                    P.mm_group(b.v(), pairs)
                    xv = x.v(slice(None), xo + tb, slice(dh * 512, dh * 512 + 512))
                    P.op(dve, [b.v(), xv], [xv],
                         lambda: nc.vector.tensor_tensor(out=xv.ap, in0=b.t[:], in1=xv.ap, op=ALU.add))
                npipe.block(xo + tb, 128 + tb * 128)
            wrelease(t, job_oc)
            wrelease(t, job_oa)

        def final_norm_block(t, x, xs, b):
            c = rms_rinv(x, xs)
            xv = x.v(slice(None), xs)
            o = ob.next()
            if id(o) not in obsem:
                obsem[id(o)] = P.new_sem("ob")
            P.op(dve, [xv, stat.v(slice(None), 2, slice(c, c + 1)), gfin.v()], [o.v()],
                 lambda: nc.vector.scalar_tensor_tensor(out=o.t[:], in0=xv.ap, scalar=stat.t[:, 2, c:c + 1],
                                                        in1=gfin.t[:], op0=ALU.mult, op1=ALU.mult))
            r0 = t * TILE + b * 128
            P.dma(sp, out_d[r0:r0 + 128, :], o.t[:], obsem[id(o)], [o.v()], [])

        def load_x(t):
            x = xb[t % 2]
            if t == 0:
                src = xin[0:640, :].rearrange("(b p) d -> p b d", p=128)
                P.dma(sp, x.t[:], src, xsem[0], [], [x.v()])
            else:
                r0 = HALO + t * TILE
                src = xin[r0:r0 + TILE, :].rearrange("(b p) d -> p b d", p=128)
                xo = 1 if t % 2 == 0 else 0
                P.dma(sp, x.t[:, xo:xo + 4, :], src, xsem[t % 2], [], [x.v(slice(None), slice(xo, xo + 4))])

        def tile_geom(t):
            xo = 1 if t % 2 == 0 else 0
            own = [(xo + b, 128 + b * 128) for b in range(4)]
            slots = ([(0, 0)] if t == 0 else []) + own
            coltiles = ([(0, 128)] if t == 0 else []) + [(128, 640)]
            return xo, own, slots, coltiles

        hsel = [0]

        def next_hT():
            h = hTs[hsel[0] % 2]
            hsel[0] += 1
            return h

        xo, own, slots, coltiles = tile_geom(0)
        hT1 = next_hT()
        for (xs, col0) in slots:
            norm_p2(norm_p1(xb[0], xs), hT1, col0, 0)
        for t in range(n_tiles):
            x = xb[t % 2]
            xo, own, slots, coltiles = tile_geom(t)
            hTm = next_hT()
            np1 = NormPipe(x, hTm, 1)
            ffn(t, 1, x, slots, coltiles, hT1, tail_cb=np1.block)
            if t + 1 < n_tiles:
                load_x(t + 1)
            hT2 = next_hT()
            np2 = NormPipe(x, hT2, 2)
            mixer(t, x, xo, hTm, np2, pre_cb=np1.flush)
            hooks = {}
            if t + 1 < n_tiles:
                xn = xb[(t + 1) % 2]
                _, _, nslots, _ = tile_geom(t + 1)
                hT1n = next_hT()
                pend = []

                def h_p1(xn=xn, nslots=nslots, pend=pend):
                    for (xs, col0) in nslots:
                        pend.append((norm_p1(xn, xs), col0))

                def h_p2(pend=pend, hT1n=hT1n):
                    for hp, col0 in pend:
                        norm_p2(hp, hT1n, col0, 0)

                hooks = {1: h_p1, 3: h_p2}
            blk = [0]

            def fin_cb(xs, col0, t=t, x=x, blk=blk):
                final_norm_block(t, x, xs, blk[0])
                blk[0] += 1

            ffn(t, 2, x, own, [(128, 640)], hT2, tail_cb=fin_cb, hooks=hooks, mid_cb=np2.flush)
            if t + 1 < n_tiles:
                hT1 = hT1n
        for s in obsem.values():
            nc.sync.wait_ge(s.h, s.count)
    return nc


def _winp_cols():
    B0, C0, U0, Q0, K0, V0 = 0, 512, 1024, 1536, 2048, 2176

    perm = list(range(0, 8)) + list(range(16, 40)) + list(range(8, 16)) + list(range(40, 64))
    cols = []
    cols += [K0 + g * 64 + perm[p] for g in range(2) for p in range(64)]
    cols += [V0 + j for j in range(128)]
    for i in range(4):
        heads = (i, 4 + i)
        cols += [Q0 + h * 64 + perm[p] for h in heads for p in range(64)]
    for i in range(4):
        cols += [C0 + i * 128 + j for j in range(128)]
        cols += [U0 + i * 128 + j for j in range(128)]
        cols += [B0 + i * 128 + j for j in range(128)]
    assert len(cols) == WINP_COLS
    return np.asarray(cols)


def _tables(half):
    pos = (np.arange(HALO + TOK_CORE, dtype=np.int64) + half * TOK_CORE - HALO)
    posf = np.maximum(pos, 0).astype(np.float64)
    inv_freq = np.float64(ROPE_THETA) ** (-(np.arange(0, 16, 2, dtype=np.float64)) / 16.0)
    ang = posf[None, :] * inv_freq[:, None]
    c8 = np.cos(ang).astype(np.float32)
    s8 = np.sin(ang).astype(np.float32)
    cos_t = np.ones((128, HALO + TOK_CORE), np.float32)
    sin_t = np.zeros((128, HALO + TOK_CORE), np.float32)
    for g in range(2):
        cos_t[g * 64:g * 64 + 8] = c8
        cos_t[g * 64 + 32:g * 64 + 40] = c8
        sin_t[g * 64:g * 64 + 8] = -s8
        sin_t[g * 64 + 32:g * 64 + 40] = s8
    return cos_t, sin_t


_NC_CACHE = {}


def kernel(x, ffn1_norm, ffn1_w_gate, ffn1_w_up, ffn1_w_down, mix_norm, w_in, conv_w, attn_sinks, w_out,
           ffn2_norm, ffn2_w_gate, ffn2_w_up, ffn2_w_down, final_norm, _n_tiles=NT_FULL, _cores=None):
    f32 = np.float32
    x = np.asarray(x, f32)
    B, S, _ = x.shape
    c = lambda a: np.ascontiguousarray(np.asarray(a, f32))
    winp = c(np.asarray(w_in, f32)[0][:, _winp_cols()])
    gcols = np.stack([np.asarray(g, f32)[0].reshape(8, 128).T for g in (ffn1_norm, mix_norm, ffn2_norm)], axis=1)
    gcols = c(gcols)
    gfin = c(np.asarray(final_norm, f32).reshape(1, D))
    convw = c(np.asarray(conv_w, f32)[0].T.reshape(4, 128, 3).transpose(1, 0, 2))
    sk = np.asarray(attn_sinks, f32)[0]
    sinks = np.zeros((128, 4), f32)
    sinks[0:64, :] = sk[None, 0:4]
    sinks[64:128, :] = sk[None, 4:8]
    kk = np.arange(128)[:, None]
    qq = np.arange(128)[None, :]
    mC = (kk <= qq).astype(f32)
    mP = (kk > qq).astype(f32)
    ident = np.eye(128, dtype=f32).astype(ml_dtypes.bfloat16)
    permT = np.eye(128, dtype=f32)[np.arange(128) ^ 32].astype(ml_dtypes.bfloat16)
    common = dict(
        wg1=c(ffn1_w_gate[0]), wu1=c(ffn1_w_up[0]), wd1=c(ffn1_w_down[0]),
        wg2=c(ffn2_w_gate[0]), wu2=c(ffn2_w_up[0]), wd2=c(ffn2_w_down[0]),
        winp=winp, wout=c(w_out[0]), gcols=gcols, gfin=gfin, convw=convw, sinks=sinks, ident=ident, permT=permT)
    tabs = [_tables(0), _tables(1)]
    in_maps = []
    cores = list(range(NCORES)) if _cores is None else _cores
    for ci in cores:
        b, half = ci // 2, ci % 2
        xi = np.zeros((HALO + TOK_CORE, D), f32)
        xi[HALO:] = x[b, half * TOK_CORE:(half + 1) * TOK_CORE]
        if half == 1:
            xi[:HALO] = x[b, TOK_CORE - HALO:TOK_CORE]
        mP0 = mP if half == 1 else np.zeros_like(mP)
        masks = np.stack([np.tile(m, (1, 4)) for m in (mC, mP, mP0)], axis=1).astype(ml_dtypes.bfloat16)
        m = dict(common)
        m.update(xin=xi, cos_t=tabs[half][0], sin_t=tabs[half][1], masks=np.ascontiguousarray(masks))
        in_maps.append(m)
    key = _n_tiles
    if key not in _NC_CACHE:
        _NC_CACHE[key] = build_program(_n_tiles)
    nc = _NC_CACHE[key]
    res = run_bass_kernel_spmd(nc, in_maps, core_ids=list(range(len(cores))))
    out = np.zeros((B, S, D), f32)
    for k, ci in enumerate(cores):
        b, half = ci // 2, ci % 2
        out[b, half * TOK_CORE:(half + 1) * TOK_CORE] = np.asarray(res.results[k]["out"], f32)
    return out
```

```python
import contextlib
import numpy as np
import ml_dtypes
import concourse.bass as bass
import concourse.mybir as mybir
from concourse.bass_utils import run_bass_kernel_spmd

F32 = mybir.dt.float32
BF16 = mybir.dt.bfloat16
AF = mybir.ActivationFunctionType
ALU = mybir.AluOpType

D = 1024
DFF = 2816
NCORES = 8
TOK_CORE = 4096
HALO = 128
TILE = 512
NT_FULL = TOK_CORE // TILE
GROUPS = [4, 4, 4, 4, 4, 2]
EPS = 1e-5
SCALE = 0.125
ROPE_THETA = 500000.0
WINP_COLS = 256 + 512 + 4 * 384


class Sem:
    def __init__(self, h):
        self.h = h
        self.count = 0


class Engine:
    def __init__(self, name, inst, sem):
        self.name = name
        self.i = inst
        self.sem = sem
        self.waited = {}


class Rec:
    __slots__ = ("box", "sem", "cnt", "w")

    def __init__(self, box, sem, cnt, w):
        self.box, self.sem, self.cnt, self.w = box, sem, cnt, w


class Buf:
    def __init__(self, name, t, shape):
        self.name = name
        self.t = t
        self.shape = tuple(shape)
        self.recs = []

    def v(self, *idx):
        idx = list(idx) + [slice(None)] * (len(self.shape) - len(idx))
        box = []
        for d, k in zip(self.shape, idx):
            if isinstance(k, int):
                box.append((k, k + 1))
            else:
                a = 0 if k.start is None else k.start
                b = d if k.stop is None else k.stop
                assert 0 <= a < b <= d, (self.name, idx, self.shape)
                box.append((a, b))
        ap = self.t[tuple(idx)] if self.t is not None else None
        return View(self, tuple(box), ap)


class View:
    def __init__(self, buf, box, ap):
        self.buf, self.box, self.ap = buf, box, ap

    def with_ap(self, ap):
        return View(self.buf, self.box, ap)


def _overlap(a, b):
    for (a0, a1), (b0, b1) in zip(a, b):
        if a1 <= b0 or b1 <= a0:
            return False
    return True


def _contains(a, b):
    for (a0, a1), (b0, b1) in zip(a, b):
        if b0 < a0 or b1 > a1:
            return False
    return True


class Prog:
    def __init__(self, nc, es):
        self.nc = nc
        self.es = es
        self.nsem = 0
        self.pe = Engine("pe", nc.tensor, self.new_sem("pe"))
        self.act = Engine("act", nc.scalar, self.new_sem("act"))
        self.dve = Engine("dve", nc.vector, self.new_sem("dve"))
        self.pool = Engine("pool", nc.gpsimd, self.new_sem("pool"))
        self.sp = Engine("sp", nc.sync, self.new_sem("sp"))
        self.sbuf_bytes = 0

    def new_sem(self, name):
        self.nsem += 1
        return Sem(self.es.enter_context(self.nc.semaphore(f"s_{name}_{self.nsem}")))

    def sbuf(self, name, shape, dt):
        t = self.es.enter_context(self.nc.sbuf_tensor("sb_" + name, list(shape), dt))
        n = 1
        for s in shape[1:]:
            n *= s
        self.sbuf_bytes += n * (4 if dt == F32 else 2)
        return Buf(name, t, shape)

    def psum(self, name, shape, dt):
        t = self.es.enter_context(self.nc.psum_tensor("ps_" + name, list(shape), dt))
        b = Buf(name, t, shape)
        b.is_psum = True
        return b

    def _sync(self, eng, reads, writes):
        need = {}
        for v in reads:
            psum = getattr(v.buf, "is_psum", False)
            for r in v.buf.recs:
                if (r.w and (psum or _overlap(r.box, v.box))) or (psum and (not r.w) and r.sem is not eng.sem):
                    if need.get(r.sem, 0) < r.cnt:
                        need[r.sem] = r.cnt
        for v in writes:
            psum = getattr(v.buf, "is_psum", False)
            for r in v.buf.recs:
                if psum or _overlap(r.box, v.box):
                    if need.get(r.sem, 0) < r.cnt:
                        need[r.sem] = r.cnt
        for s, c in need.items():
            if s is eng.sem and eng.name == "pe":
                continue
            if eng.waited.get(s, 0) >= c:
                continue
            eng.i.wait_ge(s.h, c)
            eng.waited[s] = c

    def _record(self, reads, writes, sem, cnt):
        for v in reads:
            recs = v.buf.recs
            recs[:] = [r for r in recs if not ((not r.w) and r.sem is sem and r.box == v.box)]
            recs.append(Rec(v.box, sem, cnt, False))
        for v in writes:
            recs = v.buf.recs
            recs[:] = [r for r in recs if not _contains(v.box, r.box)]
            recs.append(Rec(v.box, sem, cnt, True))

    def op(self, eng, reads, writes, fn):
        self._sync(eng, reads, writes)
        ins = fn()
        eng.sem.count += 1
        ins.then_inc(eng.sem.h, 1)
        self._record(reads, writes, eng.sem, eng.sem.count)

    def mm_group(self, out, pairs, transpose_ident=None):
        eng = self.pe
        reads = []
        for a, b in pairs:
            reads += [a, b]
        if transpose_ident is not None:
            reads.append(transpose_ident)
        self._sync(eng, reads, [out])
        n = len(pairs)
        ins = None
        for i, (a, b) in enumerate(pairs):
            ins = self.nc.tensor.matmul(out.ap, a.ap, b.ap, start=(i == 0), stop=(i == n - 1))
        eng.sem.count += 1
        ins.then_inc(eng.sem.h, 1)
        self._record(reads, [out], eng.sem, eng.sem.count)

    def tr_group(self, outs_ins, ident, outbox):
        eng = self.pe
        reads = [i for _, i in outs_ins] + [ident]
        self._sync(eng, reads, [outbox])
        ins = None
        for o, i in outs_ins:
            ins = self.nc.tensor.transpose(o.ap, i.ap, ident.ap)
        eng.sem.count += 1
        ins.then_inc(eng.sem.h, 1)
        self._record(reads, [outbox], eng.sem, eng.sem.count)

    def dma(self, qeng, out_ap, in_ap, sem, reads, writes):
        self._sync(qeng, reads, writes)
        ins = qeng.i.dma_start(out=out_ap, in_=in_ap)
        sem.count += 16
        ins.then_inc(sem.h, 16)
        self._record(reads, writes, sem, sem.count)


def dma_group(P, qeng, items, sem):
    for (out_ap, in_ap, reads, writes) in items:
        P._sync(qeng, reads, writes)
    for (out_ap, in_ap, reads, writes) in items:
        ins = qeng.i.dma_start(out=out_ap, in_=in_ap)
        sem.count += 16
        ins.then_inc(sem.h, 16)
    for (out_ap, in_ap, reads, writes) in items:
        P._record(reads, writes, sem, sem.count)


class Ring:
    def __init__(self, bufs):
        self.bufs = bufs
        self.n = 0

    def next(self):
        b = self.bufs[self.n % len(self.bufs)]
        self.n += 1
        return b


def build_program(n_tiles=NT_FULL):
    nc = bass.Bass("TRN2", target_bir_lowering=False)

    def din(name, shape, dt=F32):
        return nc.dram_tensor(name, list(shape), dt, kind="ExternalInput").ap()

    xin = din("xin", [HALO + TOK_CORE, D])
    w_ffn = {}
    for k in (1, 2):
        w_ffn[k] = (din(f"wg{k}", [D, DFF]), din(f"wu{k}", [D, DFF]), din(f"wd{k}", [DFF, D]))
    winp = din("winp", [D, WINP_COLS])
    wout = din("wout", [D, D])
    gcols_d = din("gcols", [128, 3, 8])
    gfin_d = din("gfin", [1, D])
    convw_d = din("convw", [128, 4, 3])
    sinks_d = din("sinks", [128, 4])
    cos_d = din("cos_t", [128, HALO + TOK_CORE])
    sin_d = din("sin_t", [128, HALO + TOK_CORE])
    masks_d = din("masks", [128, 3, 512], BF16)
    ident_d = din("ident", [128, 128], BF16)
    perm_d = din("permT", [128, 128], BF16)
    out_d = nc.dram_tensor("out", [TOK_CORE, D], F32, kind="ExternalOutput").ap()

    with contextlib.ExitStack() as es:
        P = Prog(nc, es)
        pe, act, dve, pool, sp = P.pe, P.act, P.dve, P.pool, P.sp

        jobs = []

        def add_job(name, ring, shape, srcs):
            scr = nc.dram_tensor(f"scr_{name}", list(shape), BF16, kind="Internal").ap()
            jobs.append(dict(name=name, ring=ring, shape=shape, srcs=srcs, scr=scr,
                             buf=Buf(f"scr_{name}", None, (1,))))
            return len(jobs) - 1

        ffn_jobs = {}
        for k in (1, 2):
            wg, wu, wd = w_ffn[k]
            wgv = wg.rearrange("(kc p) f -> p kc f", p=128)
            wuv = wu.rearrange("(kc p) f -> p kc f", p=128)
            wdv = wd.rearrange("(j p) d -> p j d", p=128)
            lst = []
            c0 = 0
            for gi, nch in enumerate(GROUPS):
                cols = nch * 128
                jg = add_job(f"g{k}_{gi}", "A", (128, 8, cols), [((0, 128), wgv[:, :, c0 * 128:c0 * 128 + cols])])
                ju = add_job(f"u{k}_{gi}", "A", (128, 8, cols), [((0, 128), wuv[:, :, c0 * 128:c0 * 128 + cols])])
                jd = add_job(f"d{k}_{gi}", "D", (128, nch, D), [((0, 128), wdv[:, c0:c0 + nch, :])])
                lst.append((jg, ju, jd, nch))
                c0 += nch
            ffn_jobs[k] = lst
        winv = winp.rearrange("(kc p) f -> p kc f", p=128)
        mix_jobs = []
        mcols = [256, 512, 384, 384, 384, 384]
        c0 = 0
        for mi, cols in enumerate(mcols):
            mix_jobs.append(add_job(f"m{mi}", "A", (128, 8, cols), [((0, 128), winv[:, :, c0:c0 + cols])]))
            c0 += cols
        assert c0 == WINP_COLS
        woc = wout[0:512, :].rearrange("(j p) d -> p j d", p=128)
        job_oc = add_job("oc", "D", (128, 4, D), [((0, 128), woc)])
        wo_a0 = wout[512:768, :].rearrange("(c p) d -> p c d", p=64)
        wo_a1 = wout[768:1024, :].rearrange("(c p) d -> p c d", p=64)
        job_oa = add_job("oa", "D", (128, 4, D), [((0, 64), wo_a0), ((64, 128), wo_a1)])

        order = []
        for k in (1, 2):
            lst = ffn_jobs[k]
            seq = []
            for gi, (jg, ju, jd, nch) in enumerate(lst):
                seq += [jg, ju]
                if gi > 0:
                    seq.append(lst[gi - 1][2])
            seq.append(lst[-1][2])
            if k == 1:
                order += seq + mix_jobs + [job_oc, job_oa]
            else:
                order += seq
        for j in order:
            jobs[j]["csem"] = P.new_sem(f"cv{j}")

        ringA = [P.sbuf(f"rA{i}", (128, 8, 512), BF16) for i in range(4)]
        ringD = [P.sbuf(f"rD{i}", (128, 4, D), BF16) for i in range(2)]
        ring_sems = {id(b): P.new_sem("ld") for b in ringA + ringD}
        ring_sems_sw = {id(b): P.new_sem("ldsw") for b in ringA + ringD}
        xb = [P.sbuf("xb0", (128, 5, D), F32), P.sbuf("xb1", (128, 4, D), F32)]
        xsem = [P.new_sem("x0"), P.new_sem("x1")]
        hTs = [P.sbuf(f"hT{i}", (128, 8, 640), BF16) for i in range(2)]
        hpre = Ring([P.sbuf(f"hpre{i}", (128, D), BF16) for i in range(5)])
        AT = [P.sbuf(f"AT{i}", (128, 4, 640), BF16) for i in range(2)]
        stmp = Ring([P.sbuf(f"stmp{i}", (128, 512), F32) for i in range(3)])
        junk = P.sbuf("junk", (128, D), BF16)
        stat = P.sbuf("stat", (128, 3, 64), F32)
        gfin = P.sbuf("gfin", (128, D), F32)
        gcols = P.sbuf("gcols", (128, 3, 8), F32)
        epsb = P.sbuf("epsb", (128, 1), F32)
        ob = Ring([P.sbuf(f"ob{i}", (128, D), F32) for i in range(2)])
        obsem = {}
        ident = P.sbuf("ident", (128, 128), BF16)
        ones = P.sbuf("ones", (128, 128), BF16)
        masks = P.sbuf("masks", (128, 3, 512), BF16)
        convw = P.sbuf("convw", (128, 4, 3), F32)
        sinks = P.sbuf("sinks", (128, 4), F32)
        esink = P.sbuf("esink", (128, 4), F32)
        cs = P.sbuf("cs", (128, 2, 640), F32)
        cssem = P.new_sem("cs")
        qrot = P.sbuf("qrot", (128, 4, 512), BF16)
        krot = P.sbuf("krot", (128, 640), BF16)
        Vt = P.sbuf("Vt", (128, 5, 128), BF16)
        yA = P.sbuf("yA", (128, 4, 512), BF16)
        yB = P.sbuf("yB", (128, 4, 512), BF16)
        cu = [P.sbuf(f"cu{i}", (128, 514), F32) for i in range(4)]
        ytmp = Ring([P.sbuf(f"ytmp{i}", (128, 512), F32) for i in range(1)])
        csb = Ring([P.sbuf(f"csb{i}", (128, 512), F32) for i in range(2)])
        csbh = P.sbuf("csbh", (128, 2), F32)
        rt1 = Ring([P.sbuf(f"rt1_{i}", (128, 512), F32) for i in range(1)])
        rt2 = Ring([P.sbuf(f"rt2_{i}", (128, 512), F32) for i in range(1)])
        qbf = Ring([P.sbuf(f"qbf{i}", (128, 512), BF16) for i in range(2)])
        permT = P.sbuf("permT", (128, 128), BF16)
        Eb = Ring([P.sbuf(f"E{i}", (128, 512), BF16) for i in range(4)])
        Pb = Ring([P.sbuf(f"P{i}", (128, 512), BF16) for i in range(8)])
        rden = Ring([P.sbuf(f"rden{i}", (128, 512), F32) for i in range(2)])
        psF = [P.psum(f"psF{i}", (128, 512), F32) for i in range(6)]
        psT = [P.psum(f"psT{i}", (128, 8, 128), BF16) for i in range(2)]
        banks = Ring(psF)
        tbanks = Ring(psT)
        csem_c = P.new_sem("const")
        assert P.sbuf_bytes <= 206 * 1024, P.sbuf_bytes

        csem_a = P.new_sem("constA")

        def x0_block(b0):
            P.dma(sp, xb[0].t[:, b0, :], xin[b0 * 128:(b0 + 1) * 128, :], P.new_sem(f"x0b{b0}"), [],
                  [xb[0].v(slice(None), b0)])

        x0_block(0)
        dma_group(P, sp, [(buf.t[:], src, [], [buf.v()]) for buf, src in (
            (ident, ident_d[:]), (gcols, gcols_d[:]))], csem_a)
        for b0 in range(1, 5):
            x0_block(b0)
        dma_group(P, sp, [(buf.t[:], src, [], [buf.v()]) for buf, src in (
            (permT, perm_d[:]), (masks, masks_d[:]), (convw, convw_d[:]), (sinks, sinks_d[:]),
            (gfin, gfin_d[0:1, :].partition_broadcast(128)))], csem_c)
        P.op(dve, [], [epsb.v()], lambda: nc.vector.memset(epsb.t[:], EPS))
        P.op(dve, [], [ones.v()], lambda: nc.vector.memset(ones.t[:], 1.0))

        def late_consts():
            P.op(act, [sinks.v()], [esink.v()],
                 lambda: nc.scalar.activation(out=esink.t[:], in_=sinks.t[:], func=AF.Exp))


        for i in range(4):
            P.op(dve, [], [cu[i].v(slice(None), slice(0, 2))], lambda i=i: nc.vector.memset(cu[i].t[:, 0:2], 0.0))

        state = dict(pos=0, outstanding={"A": [], "D": []}, slot={}, nA=0, nD=0)
        total_jobs = [(t, j) for t in range(n_tiles) for j in order]

        def pump(limit_pos):
            while state["pos"] < min(limit_pos, len(total_jobs)):
                t, j = total_jobs[state["pos"]]
                jb = jobs[j]
                ring = ringA if jb["ring"] == "A" else ringD
                outs = state["outstanding"][jb["ring"]]
                if len(outs) >= len(ring):
                    return
                key = "nA" if jb["ring"] == "A" else "nD"
                slot = ring[state[key] % len(ring)]
                state[key] += 1
                shp = jb["shape"]
                dst = slot.v(slice(0, shp[0]), slice(0, shp[1]), slice(0, shp[2]))
                full = View(jb["buf"], ((0, 1),), None)
                late = (order.index(j) % 2 == 1) and n_tiles > 2
                cast = (t == 0) or (t == 1 and late)
                store = (t == 0 and not late and n_tiles > 1) or (t == 1 and late)
                if cast:
                    dma_group(P, pool, [(slot.t[p0:p1, 0:shp[1], 0:shp[2]], src, [], [slot.v()])
                                        for (p0, p1), src in jb["srcs"]], ring_sems_sw[id(slot)])
                    if store:
                        P.dma(sp, jb["scr"][:], dst.ap, jb["csem"], [dst], [full])
                else:
                    P.dma(sp, dst.ap, jb["scr"][:], ring_sems[id(slot)], [full], [slot.v()])
                outs.append((t, j))
                state["slot"][(t, j)] = slot
                state["pos"] += 1

        LOOKAHEAD = 6

        def wget(t, j):
            idx = total_jobs.index((t, j), max(0, state["pos"] - 8))
            pump(idx + 1 + LOOKAHEAD)
            assert (t, j) in state["slot"], ("weight job not loaded (ring full?)", t, j)
            return state["slot"][(t, j)]

        def wrelease(t, j):
            r = jobs[j]["ring"]
            state["outstanding"][r].remove((t, j))
            del state["slot"][(t, j)]

        statn = [0]

        def rms_rinv(x, xs):
            c = statn[0] % 64
            statn[0] += 1
            xv = x.v(slice(None), xs)
            P.op(act, [xv], [junk.v(), stat.v(slice(None), 0, slice(c, c + 1))],
                 lambda: nc.scalar.activation(out=junk.t[:], in_=xv.ap, func=AF.Square,
                                              accum_out=stat.t[:, 0, c:c + 1]))
            P.op(act, [stat.v(slice(None), 0, slice(c, c + 1)), epsb.v()], [stat.v(slice(None), 1, slice(c, c + 1))],
                 lambda: nc.scalar.activation(out=stat.t[:, 1, c:c + 1], in_=stat.t[:, 0, c:c + 1], func=AF.Ln,
                                              scale=1.0 / D, bias=epsb.t[:, 0:1]))
            P.op(act, [stat.v(slice(None), 1, slice(c, c + 1))], [stat.v(slice(None), 2, slice(c, c + 1))],
                 lambda: nc.scalar.activation(out=stat.t[:, 2, c:c + 1], in_=stat.t[:, 1, c:c + 1], func=AF.Exp,
                                              scale=-0.5))
            return c

        def norm_p1(x, xs):
            c = rms_rinv(x, xs)
            xv = x.v(slice(None), xs)
            hp = hpre.next()
            P.op(act, [xv, stat.v(slice(None), 2, slice(c, c + 1))], [hp.v()],
                 lambda: nc.scalar.activation(out=hp.t[:], in_=xv.ap, func=AF.Copy, scale=stat.t[:, 2, c:c + 1]))
            return hp

        def norm_p2(hp, hT, col0, gidx):
            tb = tbanks.next()
            P.tr_group([(tb.v(slice(None), kc), hp.v(slice(None), slice(kc * 128, (kc + 1) * 128)))
                        for kc in range(8)], ident.v(), tb.v())
            hv = hT.v(slice(None), slice(None), slice(col0, col0 + 128))
            gap = gcols.t[:, gidx, :].unsqueeze(2).to_broadcast([128, 8, 128])
            P.op(dve, [tb.v(), gcols.v()], [hv],
                 lambda: nc.vector.tensor_tensor(out=hv.ap, in0=tb.t[:], in1=gap, op=ALU.mult))

        class NormPipe:
            def __init__(self, x, hT, gidx):
                self.x, self.hT, self.gidx = x, hT, gidx
                self.pending = None

            def block(self, xs, col0):
                hp = norm_p1(self.x, xs)
                if self.pending is not None:
                    norm_p2(self.pending[0], self.hT, self.pending[1], self.gidx)
                self.pending = (hp, col0)

            def flush(self):
                if self.pending is not None:
                    norm_p2(self.pending[0], self.hT, self.pending[1], self.gidx)
                self.pending = None

        def ffn(t, k, x, xslots, coltiles, hT, tail_cb=None, tail_flush=None, hooks=None, mid_cb=None):
            lst = ffn_jobs[k]
            hooks = hooks or {}

            def up(gi):
                jg, ju, jd, nch = lst[gi]
                Wg = wget(t, jg)
                Wu = wget(t, ju)
                at = AT[gi % 2]
                for jj in range(nch):
                    cts = coltiles
                    if gi == 0 and jj == 0:
                        cts = []
                        for (c0, c1) in coltiles:
                            if c1 - c0 == 512:
                                cts += [(c0, c0 + 256), (c0 + 256, c1)]
                            else:
                                cts.append((c0, c1))
                    for ci, (c0, c1) in enumerate(cts):
                        if gi == 0 and jj == 0 and mid_cb is not None and ci == len(cts) - 1:
                            mid_cb()
                        n = c1 - c0
                        bg = banks.next()
                        P.mm_group(bg.v(slice(None), slice(0, n)),
                                   [(Wg.v(slice(None), kc, slice(jj * 128, jj * 128 + 128)),
                                     hT.v(slice(None), kc, slice(c0, c1))) for kc in range(8)])
                        bu = banks.next()
                        P.mm_group(bu.v(slice(None), slice(0, n)),
                                   [(Wu.v(slice(None), kc, slice(jj * 128, jj * 128 + 128)),
                                     hT.v(slice(None), kc, slice(c0, c1))) for kc in range(8)])
                        s = stmp.next()
                        bgv = bg.v(slice(None), slice(0, n))
                        buv = bu.v(slice(None), slice(0, n))
                        sv = s.v(slice(None), slice(0, n))
                        P.op(act, [bgv], [sv], lambda: nc.scalar.activation(out=sv.ap, in_=bgv.ap, func=AF.Silu))
                        av = at.v(slice(None), jj, slice(c0, c1))
                        P.op(dve, [sv, buv], [av],
                             lambda: nc.vector.tensor_tensor(out=av.ap, in0=buv.ap, in1=sv.ap, op=ALU.mult))
                wrelease(t, jg)
                wrelease(t, ju)

            def down(gi):
                jg, ju, jd, nch = lst[gi]
                Wd = wget(t, jd)
                at = AT[gi % 2]
                for (xs, col0) in xslots:
                    for dh in range(2):
                        b = banks.next()
                        P.mm_group(b.v(), [(at.v(slice(None), jj, slice(col0, col0 + 128)),
                                            Wd.v(slice(None), jj, slice(dh * 512, dh * 512 + 512))) for jj in range(nch)])
                        xv = x.v(slice(None), xs, slice(dh * 512, dh * 512 + 512))
                        P.op(dve, [b.v(), xv], [xv],
                             lambda: nc.vector.scalar_tensor_tensor(out=xv.ap, in0=b.t[:], scalar=0.5, in1=xv.ap,
                                                                    op0=ALU.mult, op1=ALU.add))
                wrelease(t, jd)

            def down_tail(g0, g1):
                n0, n1 = lst[g0][3], lst[g1][3]
                W0 = wget(t, lst[g0][2])
                W1 = wget(t, lst[g1][2])
                a0, a1 = AT[g0 % 2], AT[g1 % 2]
                for (xs, col0) in xslots:
                    for dh in range(2):
                        b = banks.next()
                        pairs = [(a0.v(slice(None), jj, slice(col0, col0 + 128)),
                                  W0.v(slice(None), jj, slice(dh * 512, dh * 512 + 512))) for jj in range(n0)]
                        pairs += [(a1.v(slice(None), jj, slice(col0, col0 + 128)),
                                   W1.v(slice(None), jj, slice(dh * 512, dh * 512 + 512))) for jj in range(n1)]
                        P.mm_group(b.v(), pairs)
                        xv = x.v(slice(None), xs, slice(dh * 512, dh * 512 + 512))
                        P.op(dve, [b.v(), xv], [xv],
                             lambda: nc.vector.scalar_tensor_tensor(out=xv.ap, in0=b.t[:], scalar=0.5, in1=xv.ap,
                                                                    op0=ALU.mult, op1=ALU.add))
                    if tail_cb is not None:
                        tail_cb(xs, col0)
                if tail_flush is not None:
                    tail_flush()
                wrelease(t, lst[g0][2])
                wrelease(t, lst[g1][2])

            ng = len(lst)
            for gi in range(ng):
                up(gi)
                if gi in hooks:
                    hooks[gi]()
                if 0 < gi < ng - 1:
                    down(gi - 1)
            down_tail(ng - 2, ng - 1)

        def rotary_a(bq, n):
            qb = qbf.next()
            P.op(act, [bq.v(slice(None), slice(0, n))], [qb.v(slice(None), slice(0, n))],
                 lambda: nc.scalar.copy(out=qb.t[:, 0:n], in_=bq.t[:, 0:n]))
            return qb

        def rotary_b(bq, qb, c0, n, outv):
            b2 = banks.next()
            b2v = b2.v(slice(None), slice(0, n))
            P.mm_group(b2v, [(permT.v(), qb.v(slice(None), slice(0, n)))])
            t1 = rt1.next()
            t2 = rt2.next()
            bqv = bq.v(slice(None), slice(0, n))
            cv = cs.v(slice(None), 0, slice(c0, c0 + n))
            sv = cs.v(slice(None), 1, slice(c0, c0 + n))
            t1v = t1.v(slice(None), slice(0, n))
            t2v = t2.v(slice(None), slice(0, n))
            P.op(dve, [bqv, cv, qb.v(slice(None), slice(0, n))], [t1v], lambda: nc.vector.tensor_tensor(out=t1v.ap, in0=bqv.ap, in1=cv.ap, op=ALU.mult))
            P.op(dve, [b2v, sv], [t2v], lambda: nc.vector.tensor_tensor(out=t2v.ap, in0=b2v.ap, in1=sv.ap, op=ALU.mult))
            P.op(dve, [t1v, t2v], [outv], lambda: nc.vector.tensor_tensor(out=outv.ap, in0=t1v.ap, in1=t2v.ap, op=ALU.add))

        def inproj_chunk(W, coff, c0, c1, hT):
            b = banks.next()
            P.mm_group(b.v(slice(None), slice(0, c1 - c0)),
                       [(W.v(slice(None), kc, slice(coff, coff + 128)), hT.v(slice(None), kc, slice(c0, c1)))
                        for kc in range(8)])
            return b

        def mixer(t, x, xo, hT, npipe, pre_cb=None):
            first = (t == 0)
            if not first:
                P.op(pool, [krot.v(slice(None), slice(512, 640))], [krot.v(slice(None), slice(0, 128))],
                     lambda: nc.gpsimd.tensor_copy(out=krot.t[:, 0:128], in_=krot.t[:, 512:640]))
                P.op(pool, [Vt.v(slice(None), 4)], [Vt.v(slice(None), 0)],
                     lambda: nc.gpsimd.tensor_copy(out=Vt.t[:, 0, :], in_=Vt.t[:, 4, :]))
            dma_group(P, sp, [(cs.t[:, 0, :], cos_d[:, t * TILE:t * TILE + 640], [], [cs.v(slice(None), 0)]),
                              (cs.t[:, 1, :], sin_d[:, t * TILE:t * TILE + 640], [], [cs.v(slice(None), 1)])], cssem)
            coltiles = ([(0, 128)] if first else []) + [(128, 640)]
            W = wget(t, mix_jobs[0])
            kpend = None
            for (c0, c1) in ([(0, 128)] if first else []) + [(128, 384), (384, 640)]:
                if c0 == 384 and pre_cb is not None:
                    pre_cb()
                bk = inproj_chunk(W, 0, c0, c1, hT)
                qb = rotary_a(bk, c1 - c0)
                if kpend is not None:
                    rotary_b(*kpend)
                kpend = (bk, qb, c0, c1 - c0, krot.v(slice(None), slice(c0, c1)))
            rotary_b(*kpend)
            vblocks = ([0] if first else []) + [1, 2, 3, 4]
            for vb in vblocks:
                b = banks.next()
                bv = b.v(slice(None), slice(0, 128))
                P.mm_group(bv, [(hT.v(slice(None), kc, slice(vb * 128, vb * 128 + 128)),
                                 W.v(slice(None), kc, slice(128, 256))) for kc in range(8)])
                vv = Vt.v(slice(None), vb)
                P.op(act, [bv], [vv], lambda: nc.scalar.copy(out=vv.ap, in_=bv.ap))
            wrelease(t, mix_jobs[0])
            W = wget(t, mix_jobs[1])
            qpend = None
            for i in range(4):
                bq = inproj_chunk(W, i * 128, 128, 640, hT)
                qb = rotary_a(bq, 512)
                if qpend is not None:
                    rotary_b(*qpend)
                qpend = (bq, qb, 128, 512, qrot.v(slice(None), i))
            rotary_b(*qpend)
            wrelease(t, mix_jobs[1])

            def attn_qk(n):
                res = []
                for g in range(2):
                    pr = slice(64 * g, 64 * g + 64)
                    qv = qrot.v(pr, slice(None), slice(n * 128, n * 128 + 128))
                    pair = []
                    for which in range(2):
                        ks0 = (n + which) * 128
                        b = banks.next()
                        bo = b.v().with_ap(b.t[:, :].rearrange("p (c q) -> p c q", c=4))
                        P.mm_group(bo, [(krot.v(pr, slice(ks0, ks0 + 128)), qv)])
                        e = Eb.next()
                        P.op(act, [b.v()], [e.v()],
                             lambda b=b, e=e: nc.scalar.activation(out=e.t[:], in_=b.t[:], func=AF.Exp, scale=SCALE))
                        pb = Pb.next()
                        mi = (2 if (first and n == 0) else 1) if which == 0 else 0
                        mv = masks.v(slice(None), mi)
                        P.op(dve, [e.v(), mv], [pb.v()],
                             lambda e=e, pb=pb, mv=mv: nc.vector.tensor_tensor(out=pb.t[:], in0=e.t[:], in1=mv.ap,
                                                                               op=ALU.mult))
                        pair.append(pb)
                    res.append(pair)
                return res

            def attn_pv(n, res):
                for g in range(2):
                    pr = slice(64 * g, 64 * g + 64)
                    M = 64 * (g + 1)
                    pp, pc = res[g]
                    bo = banks.next()
                    P.mm_group(bo.v(slice(0, M)), [(Vt.v(slice(None), n, slice(0, M)), pp.v()),
                                                   (Vt.v(slice(None), n + 1, slice(0, M)), pc.v())])
                    bd = banks.next()
                    P.mm_group(bd.v(slice(0, M)), [(ones.v(slice(None), slice(0, M)), pp.v()),
                                                   (ones.v(slice(None), slice(0, M)), pc.v())])
                    r = rden.next()
                    rv = r.v(pr)
                    for c in range(4):
                        P.op(act, [bd.v(pr, slice(c * 128, c * 128 + 128)), esink.v()], [r.v(pr, slice(c * 128, c * 128 + 128))],
                             lambda c=c: nc.scalar.activation(out=r.t[pr, c * 128:c * 128 + 128],
                                                              in_=bd.t[pr, c * 128:c * 128 + 128], func=AF.Ln,
                                                              bias=esink.t[pr, c:c + 1], scale=1.0))
                    P.op(act, [rv], [rv],
                         lambda: nc.scalar.activation(out=r.t[pr, :], in_=r.t[pr, :], func=AF.Exp, scale=-1.0))
                    yv = yB.v(pr, slice(None), slice(n * 128, n * 128 + 128))
                    P.op(dve, [bo.v(pr), rv], [yv],
                         lambda: nc.vector.tensor_tensor(out=yv.ap,
                                                         in0=bo.t[pr, :].rearrange("p (c q) -> p c q", c=4),
                                                         in1=r.t[pr, :].rearrange("p (c q) -> p c q", c=4), op=ALU.mult))

            def conv_chunk(i):
                W = wget(t, mix_jobs[2 + i])
                cui = cu[i]
                if first:
                    bch = inproj_chunk(W, 0, 0, 128, hT)
                    buh = inproj_chunk(W, 128, 0, 128, hT)
                    P.op(act, [bch.v(slice(None), slice(126, 128))], [csbh.v()],
                         lambda: nc.scalar.copy(out=csbh.t[:], in_=bch.t[:, 126:128]))
                    P.op(dve, [csbh.v(), buh.v(slice(None), slice(126, 128))], [cui.v(slice(None), slice(0, 2))],
                         lambda: nc.vector.tensor_tensor(out=cui.t[:, 0:2], in0=buh.t[:, 126:128], in1=csbh.t[:],
                                                         op=ALU.mult))
                bc = inproj_chunk(W, 0, 128, 640, hT)
                bu = inproj_chunk(W, 128, 128, 640, hT)
                bb = inproj_chunk(W, 256, 128, 640, hT)
                cb = csb.next()
                P.op(act, [bc.v()], [cb.v()], lambda: nc.scalar.copy(out=cb.t[:], in_=bc.t[:]))
                P.op(dve, [cb.v(), bu.v()], [cui.v(slice(None), slice(2, 514))],
                     lambda: nc.vector.tensor_tensor(out=cui.t[:, 2:514], in0=bu.t[:], in1=cb.t[:], op=ALU.mult))
                y = ytmp.next()
                P.op(dve, [cui.v(slice(None), slice(2, 514)), convw.v()], [y.v()],
                     lambda: nc.vector.tensor_scalar(out=y.t[:], in0=cui.t[:, 2:514], scalar1=convw.t[:, i, 2:3],
                                                     scalar2=None, op0=ALU.mult))
                for (lo, kk) in ((1, 1), (0, 0)):
                    P.op(dve, [cui.v(slice(None), slice(lo, lo + 512)), convw.v(), y.v()], [y.v()],
                         lambda: nc.vector.scalar_tensor_tensor(out=y.t[:], in0=cui.t[:, lo:lo + 512],
                                                                scalar=convw.t[:, i, kk:kk + 1], in1=y.t[:],
                                                                op0=ALU.mult, op1=ALU.add))
                yv = yA.v(slice(None), i)
                P.op(dve, [bb.v(), y.v()], [yv],
                     lambda: nc.vector.tensor_tensor(out=yv.ap, in0=bb.t[:], in1=y.t[:], op=ALU.mult))
                P.op(pool, [cui.v(slice(None), slice(512, 514))], [cui.v(slice(None), slice(0, 2))],
                     lambda: nc.gpsimd.tensor_copy(out=cui.t[:, 0:2], in_=cui.t[:, 512:514]))
                wrelease(t, mix_jobs[2 + i])

            res = attn_qk(0)
            for n in range(4):
                res_next = attn_qk(n + 1) if n + 1 < 4 else None
                conv_chunk(n)
                attn_pv(n, res)
                res = res_next

            Woc = wget(t, job_oc)
            Woa = wget(t, job_oa)
            for tb in range(4):
                for dh in range(2):
                    b = banks.next()
                    pairs = [(yA.v(slice(None), i, slice(tb * 128, tb * 128 + 128)),
                              Woc.v(slice(None), i, slice(dh * 512, dh * 512 + 512))) for i in range(4)]
                    pairs += [(yB.v(slice(None), c, slice(tb * 128, tb * 128 + 128)),
                               Woa.v(slice(None), c, slice(dh * 512, dh * 512 + 512))) for c in range(4)]
                    P.mm_group(b.v(), pairs)
                    xv = x.v(slice(None), xo + tb, slice(dh * 512, dh * 512 + 512))
                    P.op(dve, [b.v(), xv], [xv],
                         lambda: nc.vector.tensor_tensor(out=xv.ap, in0=b.t[:], in1=xv.ap, op=ALU.add))
                npipe.block(xo + tb, 128 + tb * 128)
            wrelease(t, job_oc)
            wrelease(t, job_oa)

        def final_norm_block(t, x, xs, b):
            c = rms_rinv(x, xs)
            xv = x.v(slice(None), xs)
            o = ob.next()
            if id(o) not in obsem:
                obsem[id(o)] = P.new_sem("ob")
            P.op(dve, [xv, stat.v(slice(None), 2, slice(c, c + 1)), gfin.v()], [o.v()],
                 lambda: nc.vector.scalar_tensor_tensor(out=o.t[:], in0=xv.ap, scalar=stat.t[:, 2, c:c + 1],
                                                        in1=gfin.t[:], op0=ALU.mult, op1=ALU.mult))
            r0 = t * TILE + b * 128
            P.dma(sp, out_d[r0:r0 + 128, :], o.t[:], obsem[id(o)], [o.v()], [])

        def load_x(t):
            x = xb[t % 2]
            if t == 0:
                src = xin[0:640, :].rearrange("(b p) d -> p b d", p=128)
                P.dma(sp, x.t[:], src, xsem[0], [], [x.v()])
            else:
                r0 = HALO + t * TILE
                src = xin[r0:r0 + TILE, :].rearrange("(b p) d -> p b d", p=128)
                xo = 1 if t % 2 == 0 else 0
                P.dma(sp, x.t[:, xo:xo + 4, :], src, xsem[t % 2], [], [x.v(slice(None), slice(xo, xo + 4))])

        def tile_geom(t):
            xo = 1 if t % 2 == 0 else 0
            own = [(xo + b, 128 + b * 128) for b in range(4)]
            slots = ([(0, 0)] if t == 0 else []) + own
            coltiles = ([(0, 128)] if t == 0 else []) + [(128, 640)]
            return xo, own, slots, coltiles

        hsel = [0]

        def next_hT():
            h = hTs[hsel[0] % 2]
            hsel[0] += 1
            return h

        xo, own, slots, coltiles = tile_geom(0)
        hT1 = next_hT()
        for (xs, col0) in slots:
            norm_p2(norm_p1(xb[0], xs), hT1, col0, 0)
        late_consts()
        for t in range(n_tiles):
            x = xb[t % 2]
            xo, own, slots, coltiles = tile_geom(t)
            hTm = next_hT()
            np1 = NormPipe(x, hTm, 1)
            ffn(t, 1, x, slots, coltiles, hT1, tail_cb=np1.block)
            if t + 1 < n_tiles:
                load_x(t + 1)
            hT2 = next_hT()
            np2 = NormPipe(x, hT2, 2)
            mixer(t, x, xo, hTm, np2, pre_cb=np1.flush)
            hooks = {}
            if t + 1 < n_tiles:
                xn = xb[(t + 1) % 2]
                _, _, nslots, _ = tile_geom(t + 1)
                hT1n = next_hT()
                pend = []

                def h_p1(xn=xn, nslots=nslots, pend=pend):
                    for (xs, col0) in nslots:
                        pend.append((norm_p1(xn, xs), col0))

                def h_p2(pend=pend, hT1n=hT1n):
                    for hp, col0 in pend:
                        norm_p2(hp, hT1n, col0, 0)

                hooks = {1: h_p1, 3: h_p2}
            blk = [0]

            def fin_cb(xs, col0, t=t, x=x, blk=blk):
                final_norm_block(t, x, xs, blk[0])
                blk[0] += 1

            ffn(t, 2, x, own, [(128, 640)], hT2, tail_cb=fin_cb, hooks=hooks, mid_cb=np2.flush)
            if t + 1 < n_tiles:
                hT1 = hT1n
        for s in obsem.values():
            nc.sync.wait_ge(s.h, s.count)
    return nc


def _winp_cols():
    B0, C0, U0, Q0, K0, V0 = 0, 512, 1024, 1536, 2048, 2176

    perm = list(range(0, 8)) + list(range(16, 40)) + list(range(8, 16)) + list(range(40, 64))
    cols = []
    cols += [K0 + g * 64 + perm[p] for g in range(2) for p in range(64)]
    cols += [V0 + j for j in range(128)]
    for i in range(4):
        heads = (i, 4 + i)
        cols += [Q0 + h * 64 + perm[p] for h in heads for p in range(64)]
    for i in range(4):
        cols += [C0 + i * 128 + j for j in range(128)]
        cols += [U0 + i * 128 + j for j in range(128)]
        cols += [B0 + i * 128 + j for j in range(128)]
    assert len(cols) == WINP_COLS
    return np.asarray(cols)


def _tables(half):
    pos = (np.arange(HALO + TOK_CORE, dtype=np.int64) + half * TOK_CORE - HALO)
    posf = np.maximum(pos, 0).astype(np.float64)
    inv_freq = np.float64(ROPE_THETA) ** (-(np.arange(0, 16, 2, dtype=np.float64)) / 16.0)
    ang = posf[None, :] * inv_freq[:, None]
    c8 = np.cos(ang).astype(np.float32)
    s8 = np.sin(ang).astype(np.float32)
    cos_t = np.ones((128, HALO + TOK_CORE), np.float32)
    sin_t = np.zeros((128, HALO + TOK_CORE), np.float32)
    for g in range(2):
        cos_t[g * 64:g * 64 + 8] = c8
        cos_t[g * 64 + 32:g * 64 + 40] = c8
        sin_t[g * 64:g * 64 + 8] = -s8
        sin_t[g * 64 + 32:g * 64 + 40] = s8
    return cos_t, sin_t


_NC_CACHE = {}


def kernel(x, ffn1_norm, ffn1_w_gate, ffn1_w_up, ffn1_w_down, mix_norm, w_in, conv_w, attn_sinks, w_out,
           ffn2_norm, ffn2_w_gate, ffn2_w_up, ffn2_w_down, final_norm, _n_tiles=NT_FULL, _cores=None):
    f32 = np.float32
    x = np.asarray(x, f32)
    B, S, _ = x.shape
    c = lambda a: np.ascontiguousarray(np.asarray(a, f32))
    winp = c(np.asarray(w_in, f32)[0][:, _winp_cols()])
    gcols = np.stack([np.asarray(g, f32)[0].reshape(8, 128).T for g in (ffn1_norm, mix_norm, ffn2_norm)], axis=1)
    gcols = c(gcols)
    gfin = c(np.asarray(final_norm, f32).reshape(1, D))
    convw = c(np.asarray(conv_w, f32)[0].T.reshape(4, 128, 3).transpose(1, 0, 2))
    sk = np.asarray(attn_sinks, f32)[0]
    sinks = np.zeros((128, 4), f32)
    sinks[0:64, :] = sk[None, 0:4]
    sinks[64:128, :] = sk[None, 4:8]
    kk = np.arange(128)[:, None]
    qq = np.arange(128)[None, :]
    mC = (kk <= qq).astype(f32)
    mP = (kk > qq).astype(f32)
    ident = np.eye(128, dtype=f32).astype(ml_dtypes.bfloat16)
    permT = np.eye(128, dtype=f32)[np.arange(128) ^ 32].astype(ml_dtypes.bfloat16)
    common = dict(
        wg1=c(ffn1_w_gate[0]), wu1=c(ffn1_w_up[0]), wd1=c(ffn1_w_down[0]),
        wg2=c(ffn2_w_gate[0]), wu2=c(ffn2_w_up[0]), wd2=c(ffn2_w_down[0]),
        winp=winp, wout=c(w_out[0]), gcols=gcols, gfin=gfin, convw=convw, sinks=sinks, ident=ident, permT=permT)
    tabs = [_tables(0), _tables(1)]
    in_maps = []
    cores = list(range(NCORES)) if _cores is None else _cores
    for ci in cores:
        b, half = ci // 2, ci % 2
        xi = np.zeros((HALO + TOK_CORE, D), f32)
        xi[HALO:] = x[b, half * TOK_CORE:(half + 1) * TOK_CORE]
        if half == 1:
            xi[:HALO] = x[b, TOK_CORE - HALO:TOK_CORE]
        mP0 = mP if half == 1 else np.zeros_like(mP)
        masks = np.stack([np.tile(m, (1, 4)) for m in (mC, mP, mP0)], axis=1).astype(ml_dtypes.bfloat16)
        m = dict(common)
        m.update(xin=xi, cos_t=tabs[half][0], sin_t=tabs[half][1], masks=np.ascontiguousarray(masks))
        in_maps.append(m)
    key = _n_tiles
    if key not in _NC_CACHE:
        _NC_CACHE[key] = build_program(_n_tiles)
    nc = _NC_CACHE[key]
    res = run_bass_kernel_spmd(nc, in_maps, core_ids=list(range(len(cores))))
    out = np.zeros((B, S, D), f32)
    for k, ci in enumerate(cores):
        b, half = ci // 2, ci % 2
        out[b, half * TOK_CORE:(half + 1) * TOK_CORE] = np.asarray(res.results[k]["out"], f32)
    return out
```
